# Optimizing a Trainium2 kernel written in Bass

```python
import math
import jax, jax.numpy as jnp
from jax import lax
import numpy as np


D_MODEL = 1024
BATCH = 32
SEQ = 2048
DEPTH = 4

CHUNK = 64
Q_BLOCK = 128
MIX_WIDTH = D_MODEL // 2
N_BRANCH = 3
DA_QK_DIM = 64
DA_V_DIM = 2 * DA_QK_DIM
DA_HEADS = MIX_WIDTH // DA_V_DIM
HG_EXPAND = 128
HG_HEADS = MIX_WIDTH // HG_EXPAND
HG_V_DIM = MIX_WIDTH // HG_HEADS
SB_DIM = 64
SB_HEADS = MIX_WIDTH // SB_DIM
REL_BUCKETS = 32
REL_MAX_DIST = 128
FFN_HIDDEN = -(-8 * D_MODEL // (3 * 256)) * 256
N_IN_SLICES = 10
IN_WIDTH = N_IN_SLICES * MIX_WIDTH + N_BRANCH * D_MODEL
NORM_EPS = 1e-6

kernel_name = 'hybrid_gated_diffattn_hgrn2_stickbreaking'


def rms_norm(x, g):
    xf = x.astype(jnp.float32)
    y = xf * lax.rsqrt(jnp.mean(xf * xf, axis=-1, keepdims=True) + NORM_EPS)
    return (y * g.astype(jnp.float32)).astype(x.dtype)


def t5_bucket(rel):
    half = REL_BUCKETS // 2
    max_exact = half // 2
    ret = (rel > 0).astype(jnp.int32) * half
    n = jnp.abs(rel)
    nf = jnp.maximum(n, 1).astype(jnp.float32)
    large = max_exact + (jnp.log(nf / max_exact) / math.log(REL_MAX_DIST / max_exact)
                         * (half - max_exact)).astype(jnp.int32)
    large = jnp.minimum(large, half - 1)
    return ret + jnp.where(n < max_exact, n, large)


def diff_attention(q, k, v, lam, lam_init, subln_g, rel_bias):
    B, S = q.shape[0], q.shape[1]
    pos = jnp.arange(S, dtype=jnp.int32)
    chunk_id = pos // CHUNK
    scale = DA_QK_DIM ** -0.5
    outs = []
    for q0 in range(0, S, Q_BLOCK):
        kl = q0 + Q_BLOCK
        logits = jnp.einsum('bqhcd,bkhcd->bhcqk', q[:, q0:kl], k[:, :kl]).astype(jnp.float32) * scale
        bias = rel_bias[t5_bucket(pos[None, :kl] - pos[q0:kl, None])]
        logits = logits + jnp.transpose(bias, (2, 0, 1)).astype(jnp.float32)[None, :, None]
        mask = chunk_id[None, :kl] <= chunk_id[q0:kl, None]
        p = jax.nn.softmax(jnp.where(mask, logits, -jnp.inf), axis=-1)
        attn = p[:, :, 0] - lam * p[:, :, 1]
        outs.append(jnp.einsum('bhqk,bkhd->bqhd', attn.astype(v.dtype), v[:, :kl]))
    o = jnp.concatenate(outs, axis=1)
    o = rms_norm(o, subln_g) * (1.0 - lam_init)
    return o.reshape(B, S, DA_HEADS * DA_V_DIM)


def hgrn2(f_pre, i, q, g, lb, norm_g):
    B, S, H, DK = f_pre.shape
    DV = i.shape[-1]
    f32 = jnp.float32
    lb = lb.reshape(H, DK)
    f = lb + (1.0 - lb) * jax.nn.sigmoid(f_pre.astype(f32))
    log_f = jnp.log(f)
    kgate = 1.0 - f
    qf = jax.nn.silu(q.astype(f32))
    vf = i.astype(f32)
    n = S // CHUNK

    def to_chunks(t):
        return t.reshape(B, n, CHUNK, H, t.shape[-1]).swapaxes(0, 1)

    incl = jnp.tril(jnp.ones((CHUNK, CHUNK), dtype=bool))

    def step(state, inp):
        qc, kc, vc, lfc = inp
        cum = jnp.cumsum(lfc, axis=1)
        rel = cum[:, :, None] - cum[:, None, :]
        decay = jnp.exp(jnp.where(incl[None, :, :, None, None], rel, -jnp.inf))
        scores = jnp.einsum('bthd,bshd,btshd->bhts', qc, kc, decay)
        intra = jnp.einsum('bhts,bshv->bthv', scores, vc)
        inter = jnp.einsum('bthd,bhdv->bthv', qc * jnp.exp(cum), state)
        last = cum[:, -1]
        k_dec = kc * jnp.exp(last[:, None] - cum)
        state = state * jnp.exp(last)[..., None] + jnp.einsum('bshd,bshv->bhdv', k_dec, vc)
        return state, intra + inter

    state0 = jnp.zeros((B, H, DK, DV), f32)
    _, o = lax.scan(step, state0, (to_chunks(qf), to_chunks(kgate), to_chunks(vf), to_chunks(log_f)))
    o = o.swapaxes(0, 1).reshape(B, S, H, DV)
    o = rms_norm(o, norm_g) * jax.nn.silu(g.astype(f32))
    return o.astype(i.dtype).reshape(B, S, H * DV)


def stick_breaking(q, k, v):
    B, S = q.shape[0], q.shape[1]
    pos = jnp.arange(S, dtype=jnp.int32)
    scale = SB_DIM ** -0.5
    outs = []
    for q0 in range(0, S, Q_BLOCK):
        kl = q0 + Q_BLOCK
        z = jnp.einsum('bqhd,bkhd->bhqk', q[:, q0:kl], k[:, :kl]).astype(jnp.float32) * scale
        strict = pos[None, :kl] < pos[q0:kl, None]
        log_keep = jnp.where(strict, jax.nn.log_sigmoid(-z), 0.0)
        later = lax.cumsum(log_keep, axis=3, reverse=True) - log_keep
        w = jnp.where(strict, jnp.exp(jax.nn.log_sigmoid(z) + later), 0.0)
        outs.append(jnp.einsum('bhqk,bkhd->bqhd', w.astype(v.dtype), v[:, :kl]))
    o = jnp.concatenate(outs, axis=1)
    return o.reshape(B, S, SB_HEADS * SB_DIM)


def setup_inputs(seed: int = 0) -> dict:
    key = jax.random.key(seed)
    ks = jax.random.split(key, 14)
    nrm = jax.random.normal
    f32 = jnp.float32
    return {
        'x': nrm(ks[0], (BATCH, SEQ, D_MODEL), f32),
        'norm_mix_g': 1.0 + 0.02 * nrm(ks[1], (DEPTH, D_MODEL), f32),
        'w_in': nrm(ks[2], (DEPTH, D_MODEL, IN_WIDTH), f32) * D_MODEL ** -0.5,
        'rel_bias': 0.5 * nrm(ks[3], (REL_BUCKETS, DA_HEADS), f32),
        'diff_lambda': 0.1 * nrm(ks[4], (DEPTH, 4, DA_QK_DIM), f32),
        'diff_subln_g': 1.0 + 0.02 * nrm(ks[5], (DEPTH, DA_V_DIM), f32),
        'hgrn_lb_logits': 0.5 * nrm(ks[6], (DEPTH, HG_HEADS * HG_EXPAND), f32),
        'hgrn_norm_g': 1.0 + 0.02 * nrm(ks[7], (DEPTH, HG_V_DIM), f32),
        'w_up': nrm(ks[8], (DEPTH, N_BRANCH, MIX_WIDTH, D_MODEL), f32) * MIX_WIDTH ** -0.5,
        'w_out': nrm(ks[9], (DEPTH, D_MODEL, D_MODEL), f32) * D_MODEL ** -0.5,
        'norm_ffn_g': 1.0 + 0.02 * nrm(ks[10], (DEPTH, D_MODEL), f32),
        'w_ffn_in': nrm(ks[11], (DEPTH, D_MODEL, 2 * FFN_HIDDEN), f32) * D_MODEL ** -0.5,
        'w_ffn_out': nrm(ks[12], (DEPTH, FFN_HIDDEN, D_MODEL), f32) * FFN_HIDDEN ** -0.5,
        'final_norm_g': 1.0 + 0.02 * nrm(ks[13], (D_MODEL,), f32),
    }


def reference(x, norm_mix_g, w_in, rel_bias, diff_lambda, diff_subln_g, hgrn_lb_logits,
              hgrn_norm_g, w_up, w_out, norm_ffn_g, w_ffn_in, w_ffn_out, final_norm_g):
    f32 = jnp.float32
    B, S, _ = x.shape
    lb_all = jnp.cumsum(jax.nn.softmax(hgrn_lb_logits.astype(f32), axis=0), axis=0)
    split_points = [MIX_WIDTH * j for j in range(1, N_IN_SLICES + 1)]
    for l in range(DEPTH):
        h = rms_norm(x, norm_mix_g[l])
        proj = h @ w_in[l]
        (da_q, da_k, da_v, hg_f, hg_i, hg_q, hg_g,
         sb_q, sb_k, sb_v, gate_pre) = jnp.split(proj, split_points, axis=-1)
        lam_init = 0.8 - 0.6 * math.exp(-0.3 * l)
        lq1, lk1, lq2, lk2 = diff_lambda[l].astype(f32)
        lam = jnp.exp(jnp.sum(lq1 * lk1)) - jnp.exp(jnp.sum(lq2 * lk2)) + lam_init
        y_da = diff_attention(da_q.reshape(B, S, DA_HEADS, 2, DA_QK_DIM),
                              da_k.reshape(B, S, DA_HEADS, 2, DA_QK_DIM),
                              da_v.reshape(B, S, DA_HEADS, DA_V_DIM),
                              lam, lam_init, diff_subln_g[l], rel_bias)
        lb = lb_all[l] - lb_all[0]
        y_hg = hgrn2(hg_f.reshape(B, S, HG_HEADS, HG_EXPAND),
                     hg_i.reshape(B, S, HG_HEADS, HG_V_DIM),
                     hg_q.reshape(B, S, HG_HEADS, HG_EXPAND),
                     hg_g.reshape(B, S, HG_HEADS, HG_V_DIM),
                     lb, hgrn_norm_g[l])
        y_sb = stick_breaking(sb_q.reshape(B, S, SB_HEADS, SB_DIM),
                              sb_k.reshape(B, S, SB_HEADS, SB_DIM),
                              sb_v.reshape(B, S, SB_HEADS, SB_DIM))
        branches = jnp.stack([y_da, y_hg, y_sb], axis=2)
        up = jnp.einsum('bsnw,nwd->bsnd', branches, w_up[l])
        gates = jax.nn.sigmoid(gate_pre.reshape(B, S, N_BRANCH, D_MODEL))
        x = x + jnp.sum(gates * up, axis=2) @ w_out[l]
        h = rms_norm(x, norm_ffn_g[l])
        a, b = jnp.split(h @ w_ffn_in[l], 2, axis=-1)
        x = x + (jax.nn.silu(a) * b) @ w_ffn_out[l]
    return rms_norm(x, final_norm_g)
```

```python
import math
import numpy as np
import concourse.bass as bass
import concourse.mybir as mybir
from concourse.bass_utils import run_bass_kernel_spmd
from concourse.alu_op_type import AluOpType as ALU

F32 = mybir.dt.float32
BF16 = mybir.dt.bfloat16
AF = mybir.ActivationFunctionType
AX = mybir.AxisListType

S = 2048
D = 1024
DEPTH = 4
NCORES = 8
FFN_H = 2816
NHC = 22
EPS = 1e-6
ENGS = ("pe", "act", "dve", "pool", "sp")


class Prog:
    def __init__(self):
        self.ops = {e: [] for e in ENGS}
        self.last_w = {}
        self.readers = {}
        self.dma_cnt = {}
        self.pending = {}

    def add(self, eng, fn, reads=(), writes=(), dma=None):
        if dma is None:
            tok = (eng, len(self.ops[eng]))
        else:
            c = self.dma_cnt.get(dma, 0) + 1
            self.dma_cnt[dma] = c
            tok = ("dma:" + dma, c)
        deps = {}

        def need(t, raw):
            k, i = t
            if k == eng and (eng == "pe" or eng == "sp"):
                return
            if deps.get(k, -1) < i:
                deps[k] = i

        for k, i in self.pending.items():
            if k != eng and deps.get(k, -1) < i:
                deps[k] = i
        for b in reads:
            t = self.last_w.get(b)
            if t is not None:
                need(t, True)
        for b in writes:
            t = self.last_w.get(b)
            if t is not None:
                need(t, False)
            r = self.readers.get(b)
            if r:
                for k, i in r.items():
                    need((k, i), False)
        self.ops[eng].append({"fn": fn, "deps": deps, "tok": tok, "dma": dma})
        for b in reads:
            r = self.readers.setdefault(b, {})
            if r.get(tok[0], -1) < tok[1]:
                r[tok[0]] = tok[1]
        for b in writes:
            self.last_w[b] = tok
            self.readers[b] = {}
        return tok

    def barrier(self):
        pend = {}
        for e in ENGS:
            if self.ops[e]:
                pend[e] = len(self.ops[e]) - 1
        for d, c in self.dma_cnt.items():
            pend["dma:" + d] = c
        self.pending = pend

    def emit(self, nc):
        signaled = {e: set() for e in ENGS}
        for e in ENGS:
            seen = {}
            for op in self.ops[e]:
                w = []
                for k, i in op["deps"].items():
                    if seen.get(k, -1) < i:
                        w.append((k, i))
                        seen[k] = i
                        if k in signaled:
                            signaled[k].add(i)
                op["waits"] = w
        rank = {}
        for e in ENGS:
            rank[e] = {i: r + 1 for r, i in enumerate(sorted(signaled[e]))}
        import contextlib
        with contextlib.ExitStack() as st:
            sems = {}
            for e in ENGS:
                sems[e] = st.enter_context(nc.semaphore("s_" + e))
            for d in self.dma_cnt:
                sems["dma:" + d] = st.enter_context(nc.semaphore("d_" + d))
            block = st.enter_context(nc.Block())

            def run(e, eng):
                for idx, op in enumerate(self.ops[e]):
                    for k, i in op["waits"]:
                        if k.startswith("dma:"):
                            eng.wait_ge(sems[k], 16 * i)
                        else:
                            eng.wait_ge(sems[k], rank[k][i])
                    fn = op["fn"]
                    sig = idx in rank[e]
                    if fn is None:
                        if sig:
                            eng.nop().then_inc(sems[e], 1)
                        continue
                    ins = fn(eng)
                    if op["dma"] is not None:
                        ins.then_inc(sems["dma:" + op["dma"]], 16)
                        if sig:
                            eng.nop().then_inc(sems[e], 1)
                    elif sig:
                        ins.then_inc(sems[e], 1)

            @block.tensor
            def _(eng):
                run("pe", eng)

            @block.scalar
            def _(eng):
                run("act", eng)

            @block.vector
            def _(eng):
                run("dve", eng)

            @block.gpsimd
            def _(eng):
                run("pool", eng)

            @block.sync
            def _(eng):
                run("sp", eng)


def _t5_bucket(rel):
    try:
        import jax
        import jax.numpy as jnp
        with jax.default_device(jax.devices("cpu")[0]):
            r = jnp.asarray(rel, dtype=jnp.int32)
            half, max_exact = 16, 8
            ret = (r > 0).astype(jnp.int32) * half
            n = jnp.abs(r)
            nf = jnp.maximum(n, 1).astype(jnp.float32)
            large = max_exact + (jnp.log(nf / max_exact) / math.log(128 / max_exact)
                                 * (half - max_exact)).astype(jnp.int32)
            large = jnp.minimum(large, half - 1)
            return np.asarray(ret + jnp.where(n < max_exact, n, large))
    except Exception:
        half, max_exact = 16, 8
        ret = (rel > 0).astype(np.int32) * half
        n = np.abs(rel)
        nf = np.maximum(n, 1).astype(np.float32)
        large = max_exact + (np.log(nf / np.float32(max_exact)) / np.float32(math.log(128 / max_exact))
                             * np.float32(half - max_exact)).astype(np.int32)
        large = np.minimum(large, half - 1)
        return ret + np.where(n < max_exact, n, large)


C_ID, C_IDXD, C_IDXN, C_MSKD, C_TRIS, C_UINC, C_HGM, C_RMSK, C_END = 0, 128, 256, 384, 512, 640, 768, 1280, 1792


def make_consts():
    c = np.zeros((128, C_END), np.float32)
    k = np.arange(128)[:, None]
    q = np.arange(128)[None, :]
    c[:, C_ID:C_ID + 128] = np.eye(128, dtype=np.float32)
    c[:, C_IDXD:C_IDXD + 128] = _t5_bucket((k - q).astype(np.int32)).astype(np.float32)
    c[:, C_IDXN:C_IDXN + 128] = _t5_bucket((k - q - 128).astype(np.int32)).astype(np.float32)
    c[:, C_MSKD:C_MSKD + 128] = np.where((k >= 64) & (q < 64), -30000.0, 0.0)
    c[:, C_TRIS:C_TRIS + 128] = (k < q).astype(np.float32)
    c[:, C_UINC:C_UINC + 128] = (k >= q).astype(np.float32)
    s = np.arange(64)[:, None]
    t = np.arange(64)[None, :]
    c[0:64, C_HGM:C_HGM + 512] = np.tile((s <= t).astype(np.float32), (1, 8))
    c[:, C_RMSK:C_RMSK + 512] = np.tile((np.arange(512) % 64 != 0).astype(np.float32)[None, :], (128, 1))
    return c


def build(NSEQ=4, LAYERS=(0, 1, 2, 3), final_norm=True, dbg=None):
    nc = bass.Bass("TRN2", target_bir_lowering=False)
    P = Prog()
    L = DEPTH

    def din(name, shape):
        return nc.dram_tensor(name, list(shape), F32, kind="ExternalInput").ap()

    x = din("x", [NSEQ, S, D])
    norm_mix_g = din("norm_mix_g", [L, D])
    w_in = din("w_in", [L, D, 8192])
    rel_bias = din("rel_bias", [32, 4])
    diff_lambda = din("diff_lambda", [L, 4, 64])
    diff_subln_g = din("diff_subln_g", [L, 128])
    hgrn_lb_logits = din("hgrn_lb_logits", [L, 512])
    hgrn_norm_g = din("hgrn_norm_g", [L, 128])
    w_up = din("w_up", [L, 3, 512, D])
    w_out = din("w_out", [L, D, D])
    norm_ffn_g = din("norm_ffn_g", [L, D])
    w_ffn_in = din("w_ffn_in", [L, D, 2 * FFN_H])
    w_ffn_out = din("w_ffn_out", [L, FFN_H, D])
    final_norm_g = din("final_norm_g", [D])
    cst_d = din("cst", [128, C_END])
    out = nc.dram_tensor("out", [NSEQ, S, D], F32, kind="ExternalOutput").ap()

    def dscr(name, shape):
        return nc.dram_tensor(name, list(shape), BF16, kind="Internal").ap()

    WinS = dscr("WinS", [L, 64, 128, 1024])
    WupS = dscr("WupS", [L, 3, 128, 4096])
    WoutS = dscr("WoutS", [L, 128, 8192])
    WfiS = dscr("WfiS", [L, 44, 128, 1024])
    WfoS = dscr("WfoS", [L, 8, 128, FFN_H])

    import contextlib
    with contextlib.ExitStack() as st:
        def sb(name, shape, dt):
            return st.enter_context(nc.sbuf_tensor(name, list(shape), dt))

        XT = sb("XT", [128, 8, S], F32)
        hT = sb("hT", [128, 8, S], BF16)
        yT = sb("yT", [128, 4, S], BF16)
        CST = sb("CST", [128, C_END], F32)
        CB = sb("CB", [128, 3, 128], BF16)
        VT = sb("VT", [128, 96], F32)
        SM = sb("SM", [128, 256], F32)
        RB = sb("RB", [128, 128], F32)
        BT = sb("BT", [128, 2, 4, 2, 128], F32)
        TRI2 = sb("TRI2", [128, 2, 128], F32)
        NRM = sb("NRM", [128, 2560], BF16)
        AR = sb("AR", [128, 36864], BF16)
        PS = [st.enter_context(nc.psum_tensor("ps%d" % i, [128, 512], F32)) for i in range(8)]

        ident_f = CST[:, C_ID:C_ID + 128]
        ident_bf = CB[:, 0, :]
        ones_bf = CB[:, 1, :]
        uinc_bf = CB[:, 2, :]
        EPSC = SM[:, 0:1]
        ONEC = SM[:, 1:2]
        NEGLAM = SM[:, 8:12]
        GSDA = SM[:, 12:16]
        LBV = SM[:, 16:32]
        OMLB = SM[:, 32:48]
        LTMP = SM[:, 48:112]
        V_GMIX, V_GFFN, V_GFIN, V_SUBLN, V_HGN, V_LB = 0, 32, 64, 72, 76, 80

        class Arena:
            def __init__(self):
                self.off = 0
                self.n = 0

            def reset(self):
                self.off = 0

            def bf(self, n, name):
                assert self.off + n <= 36864, (name, self.off, n)
                a = AR[:, self.off:self.off + n]
                self.off += n
                return a

            def f32(self, n, name):
                assert self.off % 2 == 0
                assert self.off + 2 * n <= 36864, (name, self.off, n)
                a = AR[:, self.off:self.off + 2 * n].bitcast(F32)
                self.off += 2 * n
                return a

        A = Arena()

        def mm(o, lhsT, rhs, start, stop, reads, writes):
            P.add("pe", lambda e, o=o, l=lhsT, r=rhs, s0=start, s1=stop: e.matmul(o, lhsT=l, rhs=r, start=s0, stop=s1),
                  reads, writes)

        def tr(o, in_, ident, reads, writes):
            P.add("pe", lambda e, o=o, i=in_, d=ident: e.transpose(o, i, d), reads, writes)

        def act(o, in_, func, reads, writes, bias=None, scale=None):
            kw = {}
            if bias is not None:
                kw["bias"] = bias
            if scale is not None:
                kw["scale"] = scale
            P.add("act", lambda e, o=o, i=in_, f=func, kw=kw: e.activation(out=o, in_=i, func=f, **kw), reads, writes)

        def tt(eng, o, a, b, op, reads, writes):
            P.add(eng, lambda e, o=o, a=a, b=b, op=op: e.tensor_tensor(out=o, in0=a, in1=b, op=op), reads, writes)

        def ts(eng, o, a, s1, s2, op0, op1, reads, writes):
            if s2 is None:
                P.add(eng, lambda e, o=o, a=a, s1=s1, op0=op0:
                      e.tensor_scalar(out=o, in0=a, scalar1=s1, scalar2=None, op0=op0), reads, writes)
            else:
                P.add(eng, lambda e, o=o, a=a, s1=s1, s2=s2, op0=op0, op1=op1:
                      e.tensor_scalar(out=o, in0=a, scalar1=s1, scalar2=s2, op0=op0, op1=op1), reads, writes)

        def stt(o, a, s, b, op0, op1, reads, writes):
            P.add("dve", lambda e, o=o, a=a, s=s, b=b, op0=op0, op1=op1:
                  e.scalar_tensor_tensor(out=o, in0=a, scalar=s, in1=b, op0=op0, op1=op1), reads, writes)

        def cp(eng, o, in_, reads, writes):
            if eng == "act":
                P.add("act", lambda e, o=o, i=in_: e.activation(out=o, in_=i, func=AF.Copy), reads, writes)
            else:
                P.add(eng, lambda e, o=o, i=in_: e.tensor_copy(out=o, in_=i), reads, writes)

        def memset(eng, o, val, writes):
            P.add(eng, lambda e, o=o, v=val: e.memset(o, v), (), writes)

        def recip(o, in_, reads, writes):
            P.add("dve", lambda e, o=o, i=in_: e.reciprocal(out=o, in_=i), reads, writes)

        def dma(o, in_, key, reads, writes, eng="sp"):
            P.add(eng, lambda e, o=o, i=in_: e.dma_start(out=o, in_=i), reads, writes, dma=key)

        def psk(b):
            return ("ps", b)

        dma(CST[:], cst_d, "cst", (), ["CST"])
        vst = A.f32(128, "vst")
        dl = A.f32(1024, "dl")
        acc = A.f32(128, "acc")
        tmpb = A.f32(128, "tmpb")
        rows = [
            (0, 32, norm_mix_g.rearrange("l (kc p) -> (l kc) p", p=128)),
            (32, 32, norm_ffn_g.rearrange("l (kc p) -> (l kc) p", p=128)),
            (64, 8, final_norm_g.rearrange("(kc p) -> kc p", p=128)),
            (72, 4, diff_subln_g),
            (76, 4, hgrn_norm_g),
            (80, 16, hgrn_lb_logits.rearrange("l (h d) -> (l h) d", d=128)),
        ]
        for i, (r0, n, src) in enumerate(rows):
            dma(vst[r0:r0 + n, :], src, "vst%d" % i, (), [("vst", i)])
        dma(dl[:], diff_lambda.rearrange("l a d -> (l a d)").partition_broadcast(128), "dl", (), ["dl"])
        dma(RB[:], rel_bias.rearrange("b h -> (b h)").partition_broadcast(128), "rb", (), ["RB"])
        memset("dve", SM[:], 0.0, ["SM"])
        memset("dve", EPSC, EPS, ["SM"])
        memset("dve", ONEC, 1.0, ["SM"])
        cp("dve", CB[:, 0, :], CST[:, C_ID:C_ID + 128], ["CST"], ["CB"])
        memset("dve", CB[:, 1, :], 1.0, ["CB"])
        cp("dve", CB[:, 2, :], CST[:, C_UINC:C_UINC + 128], ["CST"], ["CB"])
        cp("dve", TRI2[:, 0, :], CST[:, C_TRIS:C_TRIS + 128], ["CST"], ["TRI2"])
        cp("dve", TRI2[:, 1, :], CST[:, C_TRIS:C_TRIS + 128], ["CST"], ["TRI2"])
        tr(PS[0][:, 0:96], vst[0:96, :], CST[0:96, C_ID:C_ID + 96], ["CST"] + [("vst", i) for i in range(6)], [psk(0)])
        cp("dve", VT[:], PS[0][:, 0:96], [psk(0)], ["VT"])
        lbt = VT[:, V_LB:V_LB + 16]
        e16 = LTMP[:, 0:16]
        act(e16, lbt, AF.Exp, ["VT"], ["LT"])
        s4 = LTMP[:, 16:20]
        tt("dve", s4, e16[:, 0:4], e16[:, 4:8], ALU.add, ["LT"], ["LT"])
        tt("dve", s4, s4, e16[:, 8:12], ALU.add, ["LT"], ["LT"])
        tt("dve", s4, s4, e16[:, 12:16], ALU.add, ["LT"], ["LT"])
        r4 = LTMP[:, 20:24]
        recip(r4, s4, ["LT"], ["LT"])
        for l in range(1, 4):
            tt("dve", e16[:, 4 * l:4 * l + 4], e16[:, 4 * l:4 * l + 4], r4, ALU.mult, ["LT"], ["LT"])
        cp("dve", LBV[:, 4:8], e16[:, 4:8], ["LT"], ["SM"])
        tt("dve", LBV[:, 8:12], LBV[:, 4:8], e16[:, 8:12], ALU.add, ["LT", "SM"], ["SM"])
        tt("dve", LBV[:, 12:16], LBV[:, 8:12], e16[:, 12:16], ALU.add, ["LT", "SM"], ["SM"])
        ts("dve", OMLB, LBV, -1.0, 1.0, ALU.mult, ALU.add, ["SM"], ["SM"])
        dl5 = dl.rearrange("p (l a b d) -> p l a b d", l=4, a=2, b=2)
        prod = A.f32(512, "prod")
        prod4 = prod.rearrange("p (l a d) -> p l a d", l=4, a=2)
        tt("dve", prod4, dl5[:, :, :, 0, :], dl5[:, :, :, 1, :], ALU.mult, ["dl"], ["prod"])
        sums = LTMP[:, 24:32]
        P.add("dve", lambda e: e.reduce_sum(out=sums, in_=prod.rearrange("p (g d) -> p g d", d=64), axis=AX.X),
              ["prod"], ["LT"])
        es = LTMP[:, 32:40]
        act(es, sums, AF.Exp, ["LT"], ["LT"])
        es2 = es.rearrange("p (l a) -> p l a", a=2)
        lam = LTMP[:, 40:44]
        tt("dve", lam, es2[:, :, 0], es2[:, :, 1], ALU.subtract, ["LT"], ["LT"])
        for l in range(4):
            lam_init = 0.8 - 0.6 * math.exp(-0.3 * l)
            ts("dve", NEGLAM[:, l:l + 1], lam[:, l:l + 1], lam_init, -1.0, ALU.add, ALU.mult, ["LT"], ["SM"])
            ts("dve", GSDA[:, l:l + 1], VT[:, V_SUBLN + l:V_SUBLN + l + 1], 1.0 - lam_init, None, ALU.mult, ALU.bypass,
               ["VT"], ["SM"])
        for typ, cidx, nb in ((0, C_IDXD, 32), (1, C_IDXN, 16)):
            idx = CST[:, cidx:cidx + 128]
            for h in range(4):
                for b in range(nb):
                    tgt = acc if b == 0 else tmpb
                    ts("dve", tgt, idx, float(b), RB[:, b * 4 + h:b * 4 + h + 1], ALU.is_equal, ALU.mult,
                       ["CST", "RB"], ["acc" if b == 0 else "tmpb"])
                    if b > 0:
                        tt("dve", acc, acc, tmpb, ALU.add, ["acc", "tmpb"], ["acc"])
                if typ == 0:
                    tt("dve", acc, acc, CST[:, C_MSKD:C_MSKD + 128], ALU.add, ["acc", "CST"], ["acc"])
                cp("dve", BT[:, typ, h, 0, :], acc, ["acc"], ["BT"])
                cp("dve", BT[:, typ, h, 1, :], acc, ["acc"], ["BT"])
        P.barrier()

        A.reset()
        stg_f = [A.f32(4096, "stgf%d" % i) for i in range(2)]
        stg_b = [A.bf(4096, "stgb%d" % i) for i in range(2)]
        pp = [0]
        cast_engs = ("dve", "act", "pool")

        def cast(o, in_, gcol, reads, writes, k):
            e = cast_engs[k % 3]
            if gcol is None:
                if e == "act":
                    cp("act", o, in_, reads, writes)
                else:
                    cp(e, o, in_, reads, writes)
            else:
                if e == "act":
                    act(o, in_, AF.Copy, reads + ["VT"], writes, scale=gcol)
                elif e == "dve":
                    ts("dve", o, in_, gcol, None, ALU.mult, ALU.bypass, reads + ["VT"], writes)
                else:
                    ts("pool", o, in_, gcol, 0.0, ALU.mult, ALU.add, reads + ["VT"], writes)

        def prepass_cols(src, l, gbase, nblk_total, dst):
            ngrp = nblk_total // 4
            srcv = src.rearrange("(kc p) n -> p kc n", p=128)
            for cb in range(ngrp):
                i = pp[0] % 2
                pp[0] += 1
                sf = stg_f[i].rearrange("p (kc n) -> p kc n", kc=8)
                sbv = stg_b[i].rearrange("p (b kc c) -> p b kc c", b=4, kc=8)
                dma(sf, srcv[:, :, cb * 512:(cb + 1) * 512], "stgf%d" % i, (), [("stgf", i)])
                for kc in range(8):
                    gcol = None if gbase is None else VT[:, gbase + l * 8 + kc:gbase + l * 8 + kc + 1]
                    cast(sbv[:, :, kc, :], sf[:, kc, :].rearrange("p (b c) -> p b c", b=4), gcol,
                         [("stgf", i)], [("stgb", i)], kc)
                dma(dst[cb * 4:(cb + 1) * 4].rearrange("b p f -> p b f"),
                    stg_b[i].rearrange("p (b f) -> p b f", b=4), "stgb%d" % i, [("stgb", i)], [("scr", l, i)])

        for l in (LAYERS if (dbg or {}).get('pre', True) else ()):
            prepass_cols(w_in[l], l, V_GMIX, 64, WinS[l])
            prepass_cols(w_ffn_in[l], l, V_GFFN, 44, WfiS[l])
            for n in range(3):
                i = pp[0] % 2
                pp[0] += 1
                sf = stg_f[i].rearrange("p (kc n) -> p kc n", kc=4)
                dma(sf, w_up[l, n].rearrange("(kc p) n -> p kc n", p=128), "stgf%d" % i, (), [("stgf", i)])
                for kc in range(4):
                    cast(stg_b[i][:, kc * 1024:(kc + 1) * 1024], sf[:, kc, :], None, [("stgf", i)], [("stgb", i)], kc)
                dma(WupS[l, n], stg_b[i], "stgb%d" % i, [("stgb", i)], [("scr", l, i)])
            for half in range(2):
                i = pp[0] % 2
                pp[0] += 1
                sf = stg_f[i].rearrange("p (kc n) -> p kc n", kc=8)
                dma(sf, w_out[l].rearrange("(kc p) n -> p kc n", p=128)[:, :, half * 512:(half + 1) * 512],
                    "stgf%d" % i, (), [("stgf", i)])
                sbv = stg_b[i].rearrange("p (kc n) -> p kc n", kc=8)
                for kc in range(8):
                    cast(sbv[:, kc, :], sf[:, kc, :], None, [("stgf", i)], [("stgb", i)], kc)
                dma(WoutS[l].rearrange("p (kc n) -> p kc n", kc=8)[:, :, half * 512:(half + 1) * 512], sbv,
                    "stgb%d" % i, [("stgb", i)], [("scr", l, i)])
            wfo = w_ffn_out[l].rearrange("(hc p) n -> p hc n", p=128)
            for jg in range(2):
                for h0, hn in ((0, 8), (8, 8), (16, 6)):
                    i = pp[0] % 2
                    pp[0] += 1
                    sf = stg_f[i][:, 0:hn * 512].rearrange("p (hc n) -> p hc n", hc=hn)
                    dma(sf, wfo[:, h0:h0 + hn, jg * 512:(jg + 1) * 512], "stgf%d" % i, (), [("stgf", i)])
                    sbv = stg_b[i][:, 0:4 * hn * 128].rearrange("p (b hc c) -> p b hc c", b=4, hc=hn)
                    for hc in range(hn):
                        cast(sbv[:, :, hc, :], sf[:, hc, :].rearrange("p (b c) -> p b c", b=4), None,
                             [("stgf", i)], [("stgb", i)], hc)
                    dma(WfoS[l, jg * 4:(jg + 1) * 4, :, h0 * 128:(h0 + hn) * 128].rearrange("b p f -> p b f"),
                        stg_b[i][:, 0:4 * hn * 128].rearrange("p (b f) -> p b f", b=4),
                        "stgb%d" % i, [("stgb", i)], [("scr", l, i)])
        P.barrier()
        scr_reads = [("scr", l, i) for l in LAYERS for i in range(2)] if (dbg or {}).get("pre", True) else []

        only = dbg or {}
        nrm_sq = [NRM[:, 0:512], NRM[:, 512:1024]]
        nrm_ln = NRM[:, 1024:2048].bitcast(F32)
        nrm_ex = NRM[:, 2048:2560].bitcast(F32)[:, 0:256]

        def gcols(g):
            return slice(g * 512, (g + 1) * 512)

        def rms_stats(g, nb):
            for kc in range(8):
                sq = nrm_sq[kc % 2]
                act(sq, XT[:, kc, gcols(g)], AF.Square, [("X", kc, g)], [("nsq", kc % 2)])
                mm(PS[nb][:, :], ones_bf, sq, kc == 0, kc == 7, [("nsq", kc % 2), "CB"], [psk(nb)])
            act(nrm_ln, PS[nb][:, :], AF.Ln, [psk(nb), "SM"], ["nln"], bias=EPSC, scale=1.0 / D)
            act(nrm_ln, nrm_ln, AF.Exp, ["nln"], ["nln"], scale=-0.5)

        def norm_to_hT():
            for g in range(4):
                rms_stats(g, 7)
                for kc in range(8):
                    tt("dve" if kc % 2 == 0 else "pool", hT[:, kc, gcols(g)], XT[:, kc, gcols(g)], nrm_ln, ALU.mult,
                       [("X", kc, g), "nln"], [("h", kc, g)])

        def load_x(s):
            A.reset()
            xin = [A.f32(1024, "xin%d" % i) for i in range(2)]
            for t in range(16):
                i = t % 2
                dma(xin[i], x[s, t * 128:(t + 1) * 128, :], "xin%d" % i, (), [("xin", i)])
                for hb in range(2):
                    b = (2 * t + hb) % 4
                    for k4 in range(4):
                        kc = hb * 4 + k4
                        tr(PS[b][:, k4 * 128:(k4 + 1) * 128], xin[i][:, kc * 128:(kc + 1) * 128], ident_f,
                           [("xin", i), "CST"], [psk(b)])
                    cp("act" if hb == 0 else "dve", XT[:, hb * 4:hb * 4 + 4, t * 128:(t + 1) * 128],
                       PS[b][:, :].rearrange("p (k c) -> p k c", k=4), [psk(b)],
                       [("X", hb * 4 + k4, t // 4) for k4 in range(4)])

        def store_out(s):
            A.reset()
            og = [A.f32(4096, "og%d" % i) for i in range(2)]
            on = [A.f32(512, "on%d" % i) for i in range(2)]
            for g in range(4):
                rms_stats(g, 7)
                ogv = og[g % 2].rearrange("p (t d) -> p t d", t=4)
                for kc in range(8):
                    o = on[kc % 2]
                    stt(o, XT[:, kc, gcols(g)], VT[:, V_GFIN + kc:V_GFIN + kc + 1], nrm_ln, ALU.mult, ALU.mult,
                        [("X", kc, g), "nln", "VT"], [("on", kc % 2)])
                    b = kc % 4
                    for t4 in range(4):
                        tr(PS[b][:, t4 * 128:(t4 + 1) * 128], o[:, t4 * 128:(t4 + 1) * 128], ident_f,
                           [("on", kc % 2), "CST"], [psk(b)])
                    cp("act" if kc % 2 == 0 else "dve", ogv[:, :, kc * 128:(kc + 1) * 128],
                       PS[b][:, :].rearrange("p (t c) -> p t c", t=4), [psk(b)], [("og", g % 2)])
                dma(out[s, g * 512:(g + 1) * 512, :].rearrange("(t p) d -> p t d", p=128), ogv, "og%d" % (g % 2),
                    [("og", g % 2)], [("outd", s, g)])

        def load_w(tile, src, key, name):
            dma(tile, src, key, scr_reads, [name])

        def sigm_recip(src_ps, tA, tAkey, reads):
            act(tA, src_ps, AF.Exp, reads, [tAkey], scale=-1.0)
            ts("pool", tA, tA, 1.0, 1.0, ALU.mult, ALU.add, [tAkey], [tAkey])

        def proj_fm(dstT, W, dkey, wname, banks, evac_eng="act", mode="copy", tmps=None):
            for g in range(4):
                b = banks[g % len(banks)]
                for kc in range(8):
                    mm(PS[b][:, :], W[:, kc * 128:(kc + 1) * 128], hT[:, kc, gcols(g)], kc == 0, kc == 7,
                       [wname, ("h", kc, g)], [psk(b)])
                if mode == "copy":
                    cp(evac_eng, dstT[:, gcols(g)], PS[b][:, :], [psk(b)], [(dkey, g)])
                elif mode == "split":
                    cp("act", dstT[0][0:64, gcols(g)], PS[b][0:64, :], [psk(b)], [(dkey, 0, g)])
                    cp("dve", dstT[1][64:128, gcols(g)], PS[b][64:128, :], [psk(b)], [(dkey, 1, g)])
                elif mode == "sigmoid":
                    (tA, kA), _ = tmps
                    sigm_recip(PS[b][:, :], tA, kA, [psk(b)])
                    recip(dstT[:, gcols(g)], tA, [kA], [(dkey, g)])
                else:
                    (tA, kA), (tB, kB) = tmps
                    sigm_recip(PS[b][:, :], tA, kA, [psk(b)])
                    recip(tB, tA, [kA], [kB])
                    tt("dve", dstT[:, gcols(g)], PS[b][:, :], tB, ALU.mult, [psk(b), kB], [(dkey, g)])

        def proj_tm(dst3, W, dkey, wname, banks, rows=128):
            nblk = S // rows
            for bg in range(nblk // 4):
                b = banks[bg % len(banks)]
                for b4 in range(4):
                    blk = bg * 4 + b4
                    g = (blk * rows) // 512
                    for kc in range(8):
                        mm(PS[b][0:rows, b4 * 128:(b4 + 1) * 128], hT[:, kc, blk * rows:(blk + 1) * rows],
                           W[:, kc * 128:(kc + 1) * 128], kc == 0, kc == 7, [wname, ("h", kc, g)], [psk(b)])
                yield bg, b

        def da_head(l, h):
            A.reset()
            Wq = A.bf(1024, "Wq"); Wk = A.bf(1024, "Wk"); Wv = A.bf(1024, "Wv")
            QTs = [A.bf(2048, "QT0"), A.bf(2048, "QT1")]
            KT = A.bf(2048, "KT"); Vt = A.bf(2048, "V")
            V3 = Vt.rearrange("p (k c) -> p k c", c=128)
            PTs = [A.bf(512, "PT%d" % i) for i in range(3)]
            tmpf = [A.f32(256, "tmpf%d" % i) for i in range(2)]
            Rr = A.f32(512, "Rr"); Tt = A.f32(512, "Tt"); of = A.f32(256, "of")
            sqb = A.bf(256, "sqb"); rst = A.f32(256, "rst")
            load_w(Wq, WinS[l, 0 + h], "wq", "Wq")
            load_w(Wk, WinS[l, 4 + h], "wk", "Wk")
            load_w(Wv, WinS[l, 8 + h], "wv", "Wv")
            for c in range(2):
                memset("pool", QTs[c], 0.0, [("QT", c, g) for g in range(4)])
            proj_fm(QTs, Wq, "QT", "Wq", (6, 7), mode="split")
            proj_fm(KT, Wk, "KT", "Wk", (6, 7), "dve")
            for bg, b in proj_tm(V3, Wv, "V", "Wv", (5, 6)):
                cp("act" if bg % 2 == 0 else "dve", V3[:, bg * 4:(bg + 1) * 4, :],
                   PS[b][:, :].rearrange("p (k c) -> p k c", k=4), [psk(b)], [("V", bg)])
            cbias = RB[:, 15 * 4 + h:15 * 4 + h + 1]
            it = 0
            for G in range(8):
                qg = G // 2
                last = 2 * G + 1
                for kb in range(last + 1):
                    zb = it % 3
                    pb = it % 3
                    it += 1
                    for c in range(2):
                        mm(PS[zb][:, c * 256:(c + 1) * 256], KT[:, kb * 128:(kb + 1) * 128],
                           QTs[c][:, G * 256:(G + 1) * 256], True, True,
                           [("KT", kb // 4), ("QT", c, qg)], [psk(zb)])
                    PT = PTs[pb]
                    ST4 = PS[zb][:, :].rearrange("p (c j q) -> p c j q", c=2, j=2)
                    PT4 = PT.rearrange("p (c j q) -> p c j q", c=2, j=2)
                    rels = [kb - (2 * G + j) for j in range(2)]
                    if rels[0] <= -2 and rels[1] <= -2:
                        act(PT, PS[zb][:, :], AF.Exp, [psk(zb), "RB"], [("PT", pb)], bias=cbias, scale=0.125)
                    else:
                        for j in range(2):
                            rel = rels[j]
                            if rel <= -2:
                                act(PT4[:, :, j, :], ST4[:, :, j, :], AF.Exp, [psk(zb), "RB"], [("PT", pb)],
                                    bias=cbias, scale=0.125)
                            elif rel == 1:
                                memset("pool", PT4[:, :, j, :], 0.0, [("PT", pb)])
                            else:
                                typ = 0 if rel == 0 else 1
                                tf = tmpf[j].rearrange("p (c q) -> p c q", c=2)
                                stt(tf, ST4[:, :, j, :], 0.125, BT[:, typ, h, :, :], ALU.mult, ALU.add,
                                    [psk(zb), "BT"], [("tmpf", j)])
                                act(PT4[:, :, j, :], tf, AF.Exp, [("tmpf", j)], [("PT", pb)])
                    mm(PS[3][:, :], V3[:, kb, :], PT, kb == 0, kb == last, [("V", kb // 4), ("PT", pb)], [psk(3)])
                    mm(PS[4][:, :], ones_bf, PT, kb == 0, kb == last, ["CB", ("PT", pb)], [psk(4)])
                recip(Rr, PS[4][:, :], [psk(4)], ["Rr"])
                tt("dve", Tt, PS[3][:, :], Rr, ALU.mult, [psk(3), "Rr"], ["Tt"])
                stt(of, Tt[:, 256:512], NEGLAM[:, l:l + 1], Tt[:, 0:256], ALU.mult, ALU.add, ["Tt", "SM"], ["of"])
                act(sqb, of, AF.Square, ["of"], ["sqb"])
                mm(PS[5][:, 0:256], ones_bf, sqb, True, True, ["CB", "sqb"], [psk(5)])
                act(rst, PS[5][:, 0:256], AF.Ln, [psk(5), "SM"], ["rst"], bias=EPSC, scale=1.0 / 128)
                act(rst, rst, AF.Exp, ["rst"], ["rst"], scale=-0.5)
                stt(yT[:, h, G * 256:(G + 1) * 256], of, GSDA[:, l:l + 1], rst, ALU.mult, ALU.mult,
                    ["of", "rst", "SM"], [("y", h, qg)])

        def sb_pair(l, hp):
            A.reset()
            Wq = A.bf(1024, "Wq"); Wk = A.bf(1024, "Wk"); Wv = A.bf(1024, "Wv")
            QTs = [A.bf(2048, "QT0"), A.bf(2048, "QT1")]
            KT = A.bf(2048, "KT"); Vp = A.bf(4096, "Vp")
            Vp4 = Vp.rearrange("p (k e c) -> p k e c", k=16, e=2)
            Ef = [A.f32(512, "Ef%d" % i) for i in range(2)]
            Lb = [A.bf(512, "Lb%d" % i) for i in range(2)]
            Rb = [A.bf(512, "Rb%d" % i) for i in range(2)]
            Ex = [A.f32(512, "Ex%d" % i) for i in range(2)]
            wT = [A.bf(512, "wT%d" % i) for i in range(2)]
            load_w(Wq, WinS[l, 28 + hp], "wq", "Wq")
            load_w(Wk, WinS[l, 32 + hp], "wk", "Wk")
            load_w(Wv, WinS[l, 36 + hp], "wv", "Wv")
            for c in range(2):
                memset("pool", QTs[c], 0.0, [("QT", c, g) for g in range(4)])
            proj_fm(QTs, Wq, "QT", "Wq", (6, 7), mode="split")
            proj_fm(KT, Wk, "KT", "Wk", (6, 7), "dve")
            memset("pool", Vp, 0.0, [("V", bg) for bg in range(4)])
            for bg, b in proj_tm(None, Wv, "V", "Wv", (4, 5)):
                src = PS[b][:, :].rearrange("p (k c) -> p k c", k=4)
                cp("act", Vp4[:, bg * 4:(bg + 1) * 4, 0, 0:64], src[:, :, 0:64], [psk(b)], [("V", bg)])
                cp("dve", Vp4[:, bg * 4:(bg + 1) * 4, 1, 64:128], src[:, :, 64:128], [psk(b)], [("V", bg)])
            it = 0
            for G in range(8):
                qg = G // 2
                ob = 4 + G % 2
                first = 2 * G + 1
                for kb in range(first, -1, -1):
                    i2 = it % 2
                    zb = it % 2
                    tb = 2 + it % 2
                    for e in range(2):
                        mm(PS[zb][:, e * 256:(e + 1) * 256], KT[:, kb * 128:(kb + 1) * 128],
                           QTs[e][:, G * 256:(G + 1) * 256], True, True,
                           [("KT", kb // 4), ("QT", e, qg)], [psk(zb)])
                    E = Ef[i2]
                    act(E, PS[zb][:, :], AF.Exp, [psk(zb)], [("Ef", i2)], scale=0.125)
                    E4 = E.rearrange("p (e j q) -> p e j q", e=2, j=2)
                    for j in range(2):
                        rel = kb - (2 * G + j)
                        if rel == 1:
                            memset("pool", E4[:, :, j, :], 0.0, [("Ef", i2)])
                        elif rel == 0:
                            tt("dve", E4[:, :, j, :], E4[:, :, j, :], TRI2[:], ALU.mult, [("Ef", i2), "TRI2"], [("Ef", i2)])
                    act(Lb[i2], E, AF.Ln, [("Ef", i2), "SM"], [("Lb", i2)], bias=ONEC, scale=1.0)
                    isfirst = kb == first
                    mm(PS[tb][:, :], uinc_bf, Lb[i2], True, isfirst, ["CB", ("Lb", i2)], [psk(tb)])
                    if not isfirst:
                        mm(PS[tb][:, :], ones_bf, Rb[i2], False, True, ["CB", ("Rb", i2)], [psk(tb)])
                    if kb > 0:
                        if isfirst:
                            cp("pool", Rb[1 - i2], Lb[i2], [("Lb", i2)], [("Rb", 1 - i2)])
                        else:
                            tt("pool", Rb[1 - i2], Rb[i2], Lb[i2], ALU.add, [("Lb", i2), ("Rb", i2)], [("Rb", 1 - i2)])
                    act(Ex[i2], PS[tb][:, :], AF.Exp, [psk(tb)], [("Ex", i2)], scale=-1.0)
                    tt("dve", wT[i2], E, Ex[i2], ALU.mult, [("Ef", i2), ("Ex", i2)], [("wT", i2)])
                    for e in range(2):
                        mm(PS[ob][:, 0:256], Vp4[:, kb, e, :], wT[i2][:, e * 256:(e + 1) * 256],
                           isfirst and e == 0, kb == 0 and e == 1, [("V", kb // 4), ("wT", i2)], [psk(ob)])
                    it += 1
                cp("act", yT[:, hp, G * 256:(G + 1) * 256], PS[ob][:, 0:256], [psk(ob)], [("y", hp, qg)])

        def hg_head(l, hh):
            A.reset()
            Wf = A.bf(1024, "Wf"); Wi = A.bf(1024, "Wi"); Wq = A.bf(1024, "Wq"); Wg = A.bf(1024, "Wg")
            SIG = A.f32(2048, "SIG")
            qA = A.bf(2048, "qA"); GS = A.bf(2048, "GS"); kt = A.bf(2048, "kt"); kdT = A.bf(2048, "kdT")
            Vh = A.bf(4096, "Vh")
            Vh3 = Vh.rearrange("p (c v) -> p c v", v=128)
            ELt = A.f32(32, "ELt")
            tf = [A.f32(512, "t%d" % i) for i in range(6)]
            scT = [A.bf(512, "scT%d" % i) for i in range(2)]
            kd = [A.bf(1024, "kd%d" % i) for i in range(2)]
            stf = A.f32(128, "stf")
            stb = [A.bf(128, "stb%d" % i) for i in range(2)]
            sqb = A.bf(512, "sqb"); rst = A.f32(512, "rst"); y1 = A.f32(512, "y1")
            load_w(Wf, WinS[l, 12 + hh], "wq", "Wf")
            load_w(Wi, WinS[l, 16 + hh], "wk", "Wi")
            load_w(Wq, WinS[l, 20 + hh], "wv", "Wq")
            load_w(Wg, WinS[l, 24 + hh], "wg", "Wg")
            tmps = ((tf[0], "t0"), (tf[1], "t1"))
            proj_fm(SIG, Wf, "SIG", "Wf", (0, 1), mode="sigmoid", tmps=tmps)
            proj_fm(qA, Wq, "qA", "Wq", (0, 1), mode="silu", tmps=tmps)
            proj_fm(GS, Wg, "GS", "Wg", (0, 1), mode="silu", tmps=tmps)
            for bg, b in proj_tm(None, Wi, "Vh", "Wi", (2, 3), rows=64):
                cp("dve" if bg % 2 == 0 else "act", Vh3[0:64, bg * 4:(bg + 1) * 4, :],
                   PS[b][0:64, :].rearrange("p (k c) -> p k c", k=4), [psk(b)], [("Vh", bg // 2)])
            lbc = LBV[:, l * 4 + hh:l * 4 + hh + 1]
            omc = OMLB[:, l * 4 + hh:l * 4 + hh + 1]
            for g in range(4):
                f, lf, kg, cum, Aex, Bex = tf
                ts("dve", f, SIG[:, gcols(g)], omc, lbc, ALU.mult, ALU.add, [("SIG", g), "SM"], ["t0"])
                act(lf, f, AF.Ln, ["t0"], ["t1"])
                ts("pool", kg, f, -1.0, 1.0, ALU.mult, ALU.add, ["t0"], ["t2"])
                P.add("dve", lambda e, cum=cum, lf=lf: e.tensor_tensor_scan(
                    out=cum, data0=CST[:, C_RMSK:C_RMSK + 512], data1=lf, initial=0.0, op0=ALU.mult, op1=ALU.add),
                    ["t1", "CST"], ["t3"])
                act(Aex, cum, AF.Exp, ["t3"], ["t4"])
                act(Bex, cum, AF.Exp, ["t3"], ["t5"], scale=-1.0)
                cp("pool", ELt[:, g * 8:(g + 1) * 8], Aex.rearrange("p (c t) -> p c t", t=64)[:, :, 63], ["t4"], ["ELt"])
                tt("dve", qA[:, gcols(g)], qA[:, gcols(g)], Aex, ALU.mult, [("qA", g), "t4"], [("qA", g)])
                tt("dve", kg, kg, Bex, ALU.mult, ["t2", "t5"], ["t2"])
                cp("pool", kt[:, gcols(g)], kg, ["t2"], [("kt", g)])
                for c8 in range(8):
                    c = g * 8 + c8
                    ts("pool" if c8 % 2 == 0 else "dve", kdT[:, c * 64:(c + 1) * 64], kg[:, c8 * 64:(c8 + 1) * 64],
                       ELt[:, c:c + 1], 0.0, ALU.mult, ALU.add, ["t2", "ELt"], [("kdT", g)])
            hgm = CST[0:64, C_HGM:C_HGM + 512]
            for cg in range(4):
                i2 = cg % 2
                ob = 4 + cg % 2
                for c8 in range(8):
                    c = cg * 8 + c8
                    mm(PS[2][0:64, c8 * 64:(c8 + 1) * 64], kt[:, c * 64:(c + 1) * 64], qA[:, c * 64:(c + 1) * 64],
                       True, True, [("kt", cg), ("qA", cg)], [psk(2)])
                tt("dve", scT[i2][0:64, :], PS[2][0:64, :], hgm, ALU.mult, [psk(2), "CST"], [("scT", i2)])
                psb = PS[3][:, :].bitcast(BF16)
                for c8 in range(8):
                    c = cg * 8 + c8
                    tr(psb[0:64, c8 * 128:(c8 + 1) * 128], kdT[:, c * 64:(c + 1) * 64], ident_bf,
                       [("kdT", cg), "CB"], [psk(3)])
                cp("act", kd[i2][0:64, :], psb[0:64, :], [psk(3)], [("kd", i2)])
                kd3 = kd[i2].rearrange("p (c d) -> p c d", d=128)
                for c8 in range(8):
                    c = cg * 8 + c8
                    mm(PS[ob][:, c8 * 64:(c8 + 1) * 64], Vh3[0:64, c, :], scT[i2][0:64, c8 * 64:(c8 + 1) * 64],
                       True, c == 0, [("Vh", c // 8), ("scT", i2)], [psk(ob)])
                    if c > 0:
                        mm(PS[ob][:, c8 * 64:(c8 + 1) * 64], stb[(c - 1) % 2], qA[:, c * 64:(c + 1) * 64],
                           False, True, [("stb", (c - 1) % 2), ("qA", cg)], [psk(ob)])
                    if c < 31:
                        mm(PS[6][:, 0:128], kd3[0:64, c8, :], Vh3[0:64, c, :], True, True,
                           [("kd", i2), ("Vh", c // 8)], [psk(6)])
                        if c == 0:
                            cp("dve", stf, PS[6][:, 0:128], [psk(6)], ["stf"])
                        else:
                            stt(stf, stf, ELt[:, c:c + 1], PS[6][:, 0:128], ALU.mult, ALU.add,
                                ["stf", "ELt", psk(6)], ["stf"])
                        cp("pool", stb[c % 2], stf, ["stf"], [("stb", c % 2)])
                act(sqb, PS[ob][:, :], AF.Square, [psk(ob)], ["sqb"])
                mm(PS[7][:, :], ones_bf, sqb, True, True, ["CB", "sqb"], [psk(7)])
                act(rst, PS[7][:, :], AF.Ln, [psk(7), "SM"], ["rst"], bias=EPSC, scale=1.0 / 128)
                act(rst, rst, AF.Exp, ["rst"], ["rst"], scale=-0.5)
                tt("dve", y1, PS[ob][:, :], rst, ALU.mult, [psk(ob), "rst"], ["y1"])
                stt(yT[:, hh, gcols(cg)], y1, VT[:, V_HGN + l:V_HGN + l + 1], GS[:, gcols(cg)], ALU.mult, ALU.mult,
                    ["y1", "VT", ("GS", cg)], [("y", hh, cg)])

        def merge(l, n):
            A.reset()
            Wu = A.bf(4096, "Wu"); Wg = A.bf(8192, "Wg"); Wo = A.bf(8192, "Wo")
            Wu3 = Wu.rearrange("p (kc c) -> p kc c", kc=4)
            Wg4 = Wg.rearrange("p (j kc c) -> p j kc c", j=8, kc=8)
            Wo3 = Wo.rearrange("p (kc c) -> p kc c", kc=8)
            mT = [A.bf(4096, "mT%d" % i) for i in range(2)]
            sg = [A.f32(512, "sg%d" % i) for i in range(2)]
            load_w(Wu, WupS[l, n], "wq", "Wu")
            load_w(Wg.rearrange("p (j f) -> p j f", j=8), WinS[l, 40 + 8 * n:48 + 8 * n].rearrange("b p f -> p b f"),
                   "wk", "Wg")
            load_w(Wo, WoutS[l], "wv", "Wo")
            it = 0
            for g in range(4):
                m3 = mT[g % 2].rearrange("p (j t) -> p j t", j=8)
                for j in range(8):
                    ub = it % 2
                    gb = 2 + it % 2
                    it += 1
                    for kc in range(4):
                        mm(PS[ub][:, :], Wu3[:, kc, j * 128:(j + 1) * 128], yT[:, kc, gcols(g)], kc == 0, kc == 3,
                           ["Wu", ("y", kc, g)], [psk(ub)])
                    for kc in range(8):
                        mm(PS[gb][:, :], Wg4[:, j, kc, :], hT[:, kc, gcols(g)], kc == 0, kc == 7,
                           ["Wg", ("h", kc, g)], [psk(gb)])
                    sigm_recip(PS[gb][:, :], sg[ub], ("sg", ub), [psk(gb)])
                    recip(sg[ub], sg[ub], [("sg", ub)], [("sg", ub)])
                    tt("dve", m3[:, j, :], PS[ub][:, :], sg[ub], ALU.mult, [psk(ub), ("sg", ub)], [("mT", g % 2, j)])
                for j2 in range(8):
                    ob = 4 + j2 % 2
                    for j in range(8):
                        mm(PS[ob][:, :], Wo3[:, j, j2 * 128:(j2 + 1) * 128], m3[:, j, :], j == 0, j == 7,
                           ["Wo", ("mT", g % 2, j)], [psk(ob)])
                    tt("dve", XT[:, j2, gcols(g)], XT[:, j2, gcols(g)], PS[ob][:, :], ALU.add,
                       [("X", j2, g), psk(ob)], [("X", j2, g)])

        def ffn_half(l, hf):
            A.reset()
            sT = A.bf(NHC * 1024, "sT")
            sT3 = sT.rearrange("p (hc t) -> p hc t", hc=NHC)
            Wa = [A.bf(1024, "Wa%d" % i) for i in range(2)]
            Wb = [A.bf(1024, "Wb%d" % i) for i in range(2)]
            Wo = [A.bf(FFN_H, "Wfo%d" % i) for i in range(2)]
            sg = [A.f32(512, "sg%d" % i) for i in range(2)]
            it = 0
            for hc in range(NHC):
                i2 = hc % 2
                load_w(Wa[i2], WfiS[l, hc], "wa%d" % i2, ("Wa", i2))
                load_w(Wb[i2], WfiS[l, NHC + hc], "wb%d" % i2, ("Wb", i2))
                for gg in range(2):
                    g = 2 * hf + gg
                    ab = it % 2
                    bb = 2 + it % 2
                    it += 1
                    for kc in range(8):
                        mm(PS[ab][:, :], Wa[i2][:, kc * 128:(kc + 1) * 128], hT[:, kc, gcols(g)], kc == 0, kc == 7,
                           [("Wa", i2), ("h", kc, g)], [psk(ab)])
                    for kc in range(8):
                        mm(PS[bb][:, :], Wb[i2][:, kc * 128:(kc + 1) * 128], hT[:, kc, gcols(g)], kc == 0, kc == 7,
                           [("Wb", i2), ("h", kc, g)], [psk(bb)])
                    sigm_recip(PS[ab][:, :], sg[ab], ("sg", ab), [psk(ab)])
                    recip(sg[ab], sg[ab], [("sg", ab)], [("sg", ab)])
                    tt("dve", sg[ab], PS[ab][:, :], sg[ab], ALU.mult, [psk(ab), ("sg", ab)], [("sg", ab)])
                    tt("dve", sT3[:, hc, gg * 512:(gg + 1) * 512], sg[ab], PS[bb][:, :], ALU.mult,
                       [("sg", ab), psk(bb)], [("sT", hc, gg)])
            for j in range(8):
                i2 = j % 2
                load_w(Wo[i2], WfoS[l, j], "wo%d" % i2, ("Wfo", i2))
                for gg in range(2):
                    g = 2 * hf + gg
                    ob = 4 + (2 * j + gg) % 2
                    for hc in range(NHC):
                        mm(PS[ob][:, :], Wo[i2][:, hc * 128:(hc + 1) * 128], sT3[:, hc, gg * 512:(gg + 1) * 512],
                           hc == 0, hc == NHC - 1, [("Wfo", i2), ("sT", hc, gg)], [psk(ob)])
                    tt("dve", XT[:, j, gcols(g)], XT[:, j, gcols(g)], PS[ob][:, :], ALU.add,
                       [("X", j, g), psk(ob)], [("X", j, g)])

        for s in range(NSEQ):
            load_x(s)
            P.barrier()
            for l in LAYERS:
                norm_to_hT()
                P.barrier()
                if only.get("da", True):
                    for h in range(4):
                        da_head(l, h)
                        P.barrier()
                    merge(l, 0)
                    P.barrier()
                if only.get("hg", True):
                    for hh in range(4):
                        hg_head(l, hh)
                        P.barrier()
                    merge(l, 1)
                    P.barrier()
                if only.get("sb", True):
                    for hp in range(4):
                        sb_pair(l, hp)
                        P.barrier()
                    merge(l, 2)
                    P.barrier()
                if only.get("ffn", True):
                    norm_to_hT()
                    P.barrier()
                    for hf in range(2):
                        ffn_half(l, hf)
                        P.barrier()
            store_out(s)
            P.barrier()
        P.add("sp", None, reads=[("outd", s, g) for s in range(NSEQ) for g in range(4)])
        P.emit(nc)
    return nc


_W_NAMES = ["norm_mix_g", "w_in", "rel_bias", "diff_lambda", "diff_subln_g", "hgrn_lb_logits", "hgrn_norm_g",
            "w_up", "w_out", "norm_ffn_g", "w_ffn_in", "w_ffn_out", "final_norm_g"]


def kernel(**inputs):
    x = np.ascontiguousarray(np.asarray(inputs["x"], dtype=np.float32))
    B = x.shape[0]
    nseq = B // NCORES
    shared = {k: np.ascontiguousarray(np.asarray(inputs[k], dtype=np.float32)) for k in _W_NAMES}
    shared["cst"] = make_consts()
    nc = build(NSEQ=nseq)
    in_maps = []
    for c in range(NCORES):
        m = dict(shared)
        m["x"] = x[c * nseq:(c + 1) * nseq]
        in_maps.append(m)
    res = run_bass_kernel_spmd(nc, in_maps, core_ids=list(range(NCORES)))
    return np.concatenate([np.asarray(r["out"], dtype=np.float32) for r in res.results], axis=0)
```

```python
import math
import numpy as np
import concourse.bass as bass
import concourse.mybir as mybir
from concourse.bass_utils import run_bass_kernel_spmd
from concourse.alu_op_type import AluOpType as ALU

F32 = mybir.dt.float32
BF16 = mybir.dt.bfloat16
AF = mybir.ActivationFunctionType
AX = mybir.AxisListType

S = 2048
D = 1024
DEPTH = 4
NCORES = 8
FFN_H = 2816
NHC = 22
EPS = 1e-6
ENGS = ("pe", "act", "dve", "pool", "sp")


class Prog:
    def __init__(self):
        self.ops = {e: [] for e in ENGS}
        self.last_w = {}
        self.readers = {}
        self.dma_cnt = {}
        self.pending = {}

    def add(self, eng, fn, reads=(), writes=(), dma=None):
        if dma is None:
            tok = (eng, len(self.ops[eng]))
        else:
            c = self.dma_cnt.get(dma, 0) + 1
            self.dma_cnt[dma] = c
            tok = ("dma:" + dma, c)
        deps = {}

        def need(t, raw):
            k, i = t
            if k == eng and (eng == "pe" or eng == "sp"):
                return
            if deps.get(k, -1) < i:
                deps[k] = i

        for k, i in self.pending.items():
            if k != eng and deps.get(k, -1) < i:
                deps[k] = i
        for b in reads:
            t = self.last_w.get(b)
            if t is not None:
                need(t, True)
        for b in writes:
            t = self.last_w.get(b)
            if t is not None:
                need(t, False)
            r = self.readers.get(b)
            if r:
                for k, i in r.items():
                    need((k, i), False)
        self.ops[eng].append({"fn": fn, "deps": deps, "tok": tok, "dma": dma})
        for b in reads:
            r = self.readers.setdefault(b, {})
            if r.get(tok[0], -1) < tok[1]:
                r[tok[0]] = tok[1]
        for b in writes:
            self.last_w[b] = tok
            self.readers[b] = {}
        return tok

    def barrier(self):
        pend = {}
        for e in ENGS:
            if self.ops[e]:
                pend[e] = len(self.ops[e]) - 1
        for d, c in self.dma_cnt.items():
            pend["dma:" + d] = c
        self.pending = pend

    def emit(self, nc):
        signaled = {e: set() for e in ENGS}
        for e in ENGS:
            seen = {}
            for op in self.ops[e]:
                w = []
                for k, i in op["deps"].items():
                    if seen.get(k, -1) < i:
                        w.append((k, i))
                        seen[k] = i
                        if k in signaled:
                            signaled[k].add(i)
                op["waits"] = w
        rank = {}
        for e in ENGS:
            rank[e] = {i: r + 1 for r, i in enumerate(sorted(signaled[e]))}
        import contextlib
        with contextlib.ExitStack() as st:
            sems = {}
            for e in ENGS:
                sems[e] = st.enter_context(nc.semaphore("s_" + e))
            for d in self.dma_cnt:
                sems["dma:" + d] = st.enter_context(nc.semaphore("d_" + d))
            block = st.enter_context(nc.Block())

            def run(e, eng):
                for idx, op in enumerate(self.ops[e]):
                    for k, i in op["waits"]:
                        if k.startswith("dma:"):
                            eng.wait_ge(sems[k], 16 * i)
                        else:
                            eng.wait_ge(sems[k], rank[k][i])
                    fn = op["fn"]
                    sig = idx in rank[e]
                    if fn is None:
                        if sig:
                            eng.nop().then_inc(sems[e], 1)
                        continue
                    ins = fn(eng)
                    if op["dma"] is not None:
                        ins.then_inc(sems["dma:" + op["dma"]], 16)
                        if sig:
                            eng.nop().then_inc(sems[e], 1)
                    elif sig:
                        ins.then_inc(sems[e], 1)

            @block.tensor
            def _(eng):
                run("pe", eng)

            @block.scalar
            def _(eng):
                run("act", eng)

            @block.vector
            def _(eng):
                run("dve", eng)

            @block.gpsimd
            def _(eng):
                run("pool", eng)

            @block.sync
            def _(eng):
                run("sp", eng)


def _t5_bucket(rel):
    try:
        import jax
        import jax.numpy as jnp
        with jax.default_device(jax.devices("cpu")[0]):
            r = jnp.asarray(rel, dtype=jnp.int32)
            half, max_exact = 16, 8
            ret = (r > 0).astype(jnp.int32) * half
            n = jnp.abs(r)
            nf = jnp.maximum(n, 1).astype(jnp.float32)
            large = max_exact + (jnp.log(nf / max_exact) / math.log(128 / max_exact)
                                 * (half - max_exact)).astype(jnp.int32)
            large = jnp.minimum(large, half - 1)
            return np.asarray(ret + jnp.where(n < max_exact, n, large))
    except Exception:
        half, max_exact = 16, 8
        ret = (rel > 0).astype(np.int32) * half
        n = np.abs(rel)
        nf = np.maximum(n, 1).astype(np.float32)
        large = max_exact + (np.log(nf / np.float32(max_exact)) / np.float32(math.log(128 / max_exact))
                             * np.float32(half - max_exact)).astype(np.int32)
        large = np.minimum(large, half - 1)
        return ret + np.where(n < max_exact, n, large)


C_ID, C_IDXD, C_IDXN, C_MSKD, C_TRIS, C_UINC, C_HGM, C_RMSK, C_END = 0, 128, 256, 384, 512, 640, 768, 1280, 1792


def make_consts():
    c = np.zeros((128, C_END), np.float32)
    k = np.arange(128)[:, None]
    q = np.arange(128)[None, :]
    c[:, C_ID:C_ID + 128] = np.eye(128, dtype=np.float32)
    c[:, C_IDXD:C_IDXD + 128] = _t5_bucket((k - q).astype(np.int32)).astype(np.float32)
    c[:, C_IDXN:C_IDXN + 128] = _t5_bucket((k - q - 128).astype(np.int32)).astype(np.float32)
    c[:, C_MSKD:C_MSKD + 128] = np.where((k >= 64) & (q < 64), -30000.0, 0.0)
    c[:, C_TRIS:C_TRIS + 128] = (k < q).astype(np.float32)
    c[:, C_UINC:C_UINC + 128] = (k >= q).astype(np.float32)
    s = np.arange(64)[:, None]
    t = np.arange(64)[None, :]
    c[0:64, C_HGM:C_HGM + 512] = np.tile((s <= t).astype(np.float32), (1, 8))
    c[:, C_RMSK:C_RMSK + 512] = np.tile((np.arange(512) % 64 != 0).astype(np.float32)[None, :], (128, 1))
    return c


def build(NSEQ=4, LAYERS=(0, 1, 2, 3), final_norm=True, dbg=None):
    nc = bass.Bass("TRN2", target_bir_lowering=False)
    P = Prog()
    L = DEPTH

    def din(name, shape):
        return nc.dram_tensor(name, list(shape), F32, kind="ExternalInput").ap()

    x = din("x", [NSEQ, S, D])
    norm_mix_g = din("norm_mix_g", [L, D])
    w_in = din("w_in", [L, D, 8192])
    rel_bias = din("rel_bias", [32, 4])
    diff_lambda = din("diff_lambda", [L, 4, 64])
    diff_subln_g = din("diff_subln_g", [L, 128])
    hgrn_lb_logits = din("hgrn_lb_logits", [L, 512])
    hgrn_norm_g = din("hgrn_norm_g", [L, 128])
    w_up = din("w_up", [L, 3, 512, D])
    w_out = din("w_out", [L, D, D])
    norm_ffn_g = din("norm_ffn_g", [L, D])
    w_ffn_in = din("w_ffn_in", [L, D, 2 * FFN_H])
    w_ffn_out = din("w_ffn_out", [L, FFN_H, D])
    final_norm_g = din("final_norm_g", [D])
    cst_d = din("cst", [128, C_END])
    out = nc.dram_tensor("out", [NSEQ, S, D], F32, kind="ExternalOutput").ap()

    def dscr(name, shape):
        return nc.dram_tensor(name, list(shape), BF16, kind="Internal").ap()

    WinS = dscr("WinS", [L, 64, 128, 1024])
    WupS = dscr("WupS", [L, 3, 128, 4096])
    WoutS = dscr("WoutS", [L, 128, 8192])
    WfiS = dscr("WfiS", [L, 44, 128, 1024])
    WfoS = dscr("WfoS", [L, 8, 128, FFN_H])

    import contextlib
    with contextlib.ExitStack() as st:
        def sb(name, shape, dt):
            return st.enter_context(nc.sbuf_tensor(name, list(shape), dt))

        XT = sb("XT", [128, 8, S], F32)
        hT = sb("hT", [128, 8, S], BF16)
        yT = sb("yT", [128, 4, S], BF16)
        CST = sb("CST", [128, C_END], F32)
        CB = sb("CB", [128, 3, 128], BF16)
        VT = sb("VT", [128, 96], F32)
        SM = sb("SM", [128, 256], F32)
        RB = sb("RB", [128, 128], F32)
        BT = sb("BT", [128, 2, 4, 2, 128], F32)
        TRI2 = sb("TRI2", [128, 2, 128], F32)
        NRM = sb("NRM", [128, 2560], BF16)
        AR = sb("AR", [128, 36864], BF16)
        PS = [st.enter_context(nc.psum_tensor("ps%d" % i, [128, 512], F32)) for i in range(8)]

        ident_f = CST[:, C_ID:C_ID + 128]
        ident_bf = CB[:, 0, :]
        ones_bf = CB[:, 1, :]
        uinc_bf = CB[:, 2, :]
        EPSC = SM[:, 0:1]
        ONEC = SM[:, 1:2]
        NEGLAM = SM[:, 8:12]
        GSDA = SM[:, 12:16]
        LBV = SM[:, 16:32]
        OMLB = SM[:, 32:48]
        LTMP = SM[:, 48:112]
        V_GMIX, V_GFFN, V_GFIN, V_SUBLN, V_HGN, V_LB = 0, 32, 64, 72, 76, 80

        class Arena:
            def __init__(self):
                self.off = 0
                self.n = 0

            def reset(self):
                self.off = 0

            def bf(self, n, name):
                assert self.off + n <= 36864, (name, self.off, n)
                a = AR[:, self.off:self.off + n]
                self.off += n
                return a

            def f32(self, n, name):
                assert self.off % 2 == 0
                assert self.off + 2 * n <= 36864, (name, self.off, n)
                a = AR[:, self.off:self.off + 2 * n].bitcast(F32)
                self.off += 2 * n
                return a

        A = Arena()

        def mm(o, lhsT, rhs, start, stop, reads, writes):
            P.add("pe", lambda e, o=o, l=lhsT, r=rhs, s0=start, s1=stop: e.matmul(o, lhsT=l, rhs=r, start=s0, stop=s1),
                  reads, writes)

        def tr(o, in_, ident, reads, writes):
            P.add("pe", lambda e, o=o, i=in_, d=ident: e.transpose(o, i, d), reads, writes)

        def act(o, in_, func, reads, writes, bias=None, scale=None):
            kw = {}
            if bias is not None:
                kw["bias"] = bias
            if scale is not None:
                kw["scale"] = scale
            P.add("act", lambda e, o=o, i=in_, f=func, kw=kw: e.activation(out=o, in_=i, func=f, **kw), reads, writes)

        def tt(eng, o, a, b, op, reads, writes):
            P.add(eng, lambda e, o=o, a=a, b=b, op=op: e.tensor_tensor(out=o, in0=a, in1=b, op=op), reads, writes)

        def ts(eng, o, a, s1, s2, op0, op1, reads, writes):
            if s2 is None:
                P.add(eng, lambda e, o=o, a=a, s1=s1, op0=op0:
                      e.tensor_scalar(out=o, in0=a, scalar1=s1, scalar2=None, op0=op0), reads, writes)
            else:
                P.add(eng, lambda e, o=o, a=a, s1=s1, s2=s2, op0=op0, op1=op1:
                      e.tensor_scalar(out=o, in0=a, scalar1=s1, scalar2=s2, op0=op0, op1=op1), reads, writes)

        def stt(o, a, s, b, op0, op1, reads, writes):
            P.add("dve", lambda e, o=o, a=a, s=s, b=b, op0=op0, op1=op1:
                  e.scalar_tensor_tensor(out=o, in0=a, scalar=s, in1=b, op0=op0, op1=op1), reads, writes)

        def cp(eng, o, in_, reads, writes):
            if eng == "act":
                P.add("act", lambda e, o=o, i=in_: e.activation(out=o, in_=i, func=AF.Copy), reads, writes)
            else:
                P.add(eng, lambda e, o=o, i=in_: e.tensor_copy(out=o, in_=i), reads, writes)

        def memset(eng, o, val, writes):
            P.add(eng, lambda e, o=o, v=val: e.memset(o, v), (), writes)

        def recip(o, in_, reads, writes):
            P.add("dve", lambda e, o=o, i=in_: e.reciprocal(out=o, in_=i), reads, writes)

        def dma(o, in_, key, reads, writes, eng="sp"):
            P.add(eng, lambda e, o=o, i=in_: e.dma_start(out=o, in_=i), reads, writes, dma=key)

        def psk(b):
            return ("ps", b)

        dma(CST[:], cst_d, "cst", (), ["CST"])
        vst = A.f32(128, "vst")
        dl = A.f32(1024, "dl")
        acc = A.f32(128, "acc")
        tmpb = A.f32(128, "tmpb")
        rows = [
            (0, 32, norm_mix_g.rearrange("l (kc p) -> (l kc) p", p=128)),
            (32, 32, norm_ffn_g.rearrange("l (kc p) -> (l kc) p", p=128)),
            (64, 8, final_norm_g.rearrange("(kc p) -> kc p", p=128)),
            (72, 4, diff_subln_g),
            (76, 4, hgrn_norm_g),
            (80, 16, hgrn_lb_logits.rearrange("l (h d) -> (l h) d", d=128)),
        ]
        for i, (r0, n, src) in enumerate(rows):
            dma(vst[r0:r0 + n, :], src, "vst%d" % i, (), [("vst", i)])
        dma(dl[:], diff_lambda.rearrange("l a d -> (l a d)").partition_broadcast(128), "dl", (), ["dl"])
        dma(RB[:], rel_bias.rearrange("b h -> (b h)").partition_broadcast(128), "rb", (), ["RB"])
        memset("dve", SM[:], 0.0, ["SM"])
        memset("dve", EPSC, EPS, ["SM"])
        memset("dve", ONEC, 1.0, ["SM"])
        cp("dve", CB[:, 0, :], CST[:, C_ID:C_ID + 128], ["CST"], ["CB"])
        memset("dve", CB[:, 1, :], 1.0, ["CB"])
        cp("dve", CB[:, 2, :], CST[:, C_UINC:C_UINC + 128], ["CST"], ["CB"])
        cp("dve", TRI2[:, 0, :], CST[:, C_TRIS:C_TRIS + 128], ["CST"], ["TRI2"])
        cp("dve", TRI2[:, 1, :], CST[:, C_TRIS:C_TRIS + 128], ["CST"], ["TRI2"])
        tr(PS[0][:, 0:96], vst[0:96, :], CST[0:96, C_ID:C_ID + 96], ["CST"] + [("vst", i) for i in range(6)], [psk(0)])
        cp("dve", VT[:], PS[0][:, 0:96], [psk(0)], ["VT"])
        lbt = VT[:, V_LB:V_LB + 16]
        e16 = LTMP[:, 0:16]
        act(e16, lbt, AF.Exp, ["VT"], ["LT"])
        s4 = LTMP[:, 16:20]
        tt("dve", s4, e16[:, 0:4], e16[:, 4:8], ALU.add, ["LT"], ["LT"])
        tt("dve", s4, s4, e16[:, 8:12], ALU.add, ["LT"], ["LT"])
        tt("dve", s4, s4, e16[:, 12:16], ALU.add, ["LT"], ["LT"])
        r4 = LTMP[:, 20:24]
        recip(r4, s4, ["LT"], ["LT"])
        for l in range(1, 4):
            tt("dve", e16[:, 4 * l:4 * l + 4], e16[:, 4 * l:4 * l + 4], r4, ALU.mult, ["LT"], ["LT"])
        cp("dve", LBV[:, 4:8], e16[:, 4:8], ["LT"], ["SM"])
        tt("dve", LBV[:, 8:12], LBV[:, 4:8], e16[:, 8:12], ALU.add, ["LT", "SM"], ["SM"])
        tt("dve", LBV[:, 12:16], LBV[:, 8:12], e16[:, 12:16], ALU.add, ["LT", "SM"], ["SM"])
        ts("dve", OMLB, LBV, -1.0, 1.0, ALU.mult, ALU.add, ["SM"], ["SM"])
        dl5 = dl.rearrange("p (l a b d) -> p l a b d", l=4, a=2, b=2)
        prod = A.f32(512, "prod")
        prod4 = prod.rearrange("p (l a d) -> p l a d", l=4, a=2)
        tt("dve", prod4, dl5[:, :, :, 0, :], dl5[:, :, :, 1, :], ALU.mult, ["dl"], ["prod"])
        sums = LTMP[:, 24:32]
        P.add("dve", lambda e: e.reduce_sum(out=sums, in_=prod.rearrange("p (g d) -> p g d", d=64), axis=AX.X),
              ["prod"], ["LT"])
        es = LTMP[:, 32:40]
        act(es, sums, AF.Exp, ["LT"], ["LT"])
        es2 = es.rearrange("p (l a) -> p l a", a=2)
        lam = LTMP[:, 40:44]
        tt("dve", lam, es2[:, :, 0], es2[:, :, 1], ALU.subtract, ["LT"], ["LT"])
        for l in range(4):
            lam_init = 0.8 - 0.6 * math.exp(-0.3 * l)
            ts("dve", NEGLAM[:, l:l + 1], lam[:, l:l + 1], lam_init, -1.0, ALU.add, ALU.mult, ["LT"], ["SM"])
            ts("dve", GSDA[:, l:l + 1], VT[:, V_SUBLN + l:V_SUBLN + l + 1], 1.0 - lam_init, None, ALU.mult, ALU.bypass,
               ["VT"], ["SM"])
        for typ, cidx, nb in ((0, C_IDXD, 32), (1, C_IDXN, 16)):
            idx = CST[:, cidx:cidx + 128]
            for h in range(4):
                for b in range(nb):
                    tgt = acc if b == 0 else tmpb
                    ts("dve", tgt, idx, float(b), RB[:, b * 4 + h:b * 4 + h + 1], ALU.is_equal, ALU.mult,
                       ["CST", "RB"], ["acc" if b == 0 else "tmpb"])
                    if b > 0:
                        tt("dve", acc, acc, tmpb, ALU.add, ["acc", "tmpb"], ["acc"])
                if typ == 0:
                    tt("dve", acc, acc, CST[:, C_MSKD:C_MSKD + 128], ALU.add, ["acc", "CST"], ["acc"])
                cp("dve", BT[:, typ, h, 0, :], acc, ["acc"], ["BT"])
                cp("dve", BT[:, typ, h, 1, :], acc, ["acc"], ["BT"])
        P.barrier()

        A.reset()
        stg_f = [A.f32(4096, "stgf%d" % i) for i in range(2)]
        stg_b = [A.bf(4096, "stgb%d" % i) for i in range(2)]
        pp = [0]
        cast_engs = ("dve", "act", "pool")

        def cast(o, in_, gcol, reads, writes, k):
            e = cast_engs[k % 3]
            if gcol is None:
                if e == "act":
                    cp("act", o, in_, reads, writes)
                else:
                    cp(e, o, in_, reads, writes)
            else:
                if e == "act":
                    act(o, in_, AF.Copy, reads + ["VT"], writes, scale=gcol)
                elif e == "dve":
                    ts("dve", o, in_, gcol, None, ALU.mult, ALU.bypass, reads + ["VT"], writes)
                else:
                    ts("pool", o, in_, gcol, 0.0, ALU.mult, ALU.add, reads + ["VT"], writes)

        def prepass_cols(src, l, gbase, nblk_total, dst):
            ngrp = nblk_total // 4
            srcv = src.rearrange("(kc p) n -> p kc n", p=128)
            for cb in range(ngrp):
                i = pp[0] % 2
                pp[0] += 1
                sf = stg_f[i].rearrange("p (kc n) -> p kc n", kc=8)
                sbv = stg_b[i].rearrange("p (b kc c) -> p b kc c", b=4, kc=8)
                dma(sf, srcv[:, :, cb * 512:(cb + 1) * 512], "stgf%d" % i, (), [("stgf", i)])
                for kc in range(8):
                    gcol = None if gbase is None else VT[:, gbase + l * 8 + kc:gbase + l * 8 + kc + 1]
                    cast(sbv[:, :, kc, :], sf[:, kc, :].rearrange("p (b c) -> p b c", b=4), gcol,
                         [("stgf", i)], [("stgb", i)], kc)
                dma(dst[cb * 4:(cb + 1) * 4].rearrange("b p f -> p b f"),
                    stg_b[i].rearrange("p (b f) -> p b f", b=4), "stgb%d" % i, [("stgb", i)], [("scr", l, i)])

        for l in (LAYERS if (dbg or {}).get('pre', True) else ()):
            prepass_cols(w_in[l], l, V_GMIX, 64, WinS[l])
            prepass_cols(w_ffn_in[l], l, V_GFFN, 44, WfiS[l])
            for n in range(3):
                i = pp[0] % 2
                pp[0] += 1
                sf = stg_f[i].rearrange("p (kc n) -> p kc n", kc=4)
                dma(sf, w_up[l, n].rearrange("(kc p) n -> p kc n", p=128), "stgf%d" % i, (), [("stgf", i)])
                for kc in range(4):
                    cast(stg_b[i][:, kc * 1024:(kc + 1) * 1024], sf[:, kc, :], None, [("stgf", i)], [("stgb", i)], kc)
                dma(WupS[l, n], stg_b[i], "stgb%d" % i, [("stgb", i)], [("scr", l, i)])
            for half in range(2):
                i = pp[0] % 2
                pp[0] += 1
                sf = stg_f[i].rearrange("p (kc n) -> p kc n", kc=8)
                dma(sf, w_out[l].rearrange("(kc p) n -> p kc n", p=128)[:, :, half * 512:(half + 1) * 512],
                    "stgf%d" % i, (), [("stgf", i)])
                sbv = stg_b[i].rearrange("p (kc n) -> p kc n", kc=8)
                for kc in range(8):
                    cast(sbv[:, kc, :], sf[:, kc, :], None, [("stgf", i)], [("stgb", i)], kc)
                dma(WoutS[l].rearrange("p (kc n) -> p kc n", kc=8)[:, :, half * 512:(half + 1) * 512], sbv,
                    "stgb%d" % i, [("stgb", i)], [("scr", l, i)])
            wfo = w_ffn_out[l].rearrange("(hc p) n -> p hc n", p=128)
            for jg in range(2):
                for h0, hn in ((0, 8), (8, 8), (16, 6)):
                    i = pp[0] % 2
                    pp[0] += 1
                    sf = stg_f[i][:, 0:hn * 512].rearrange("p (hc n) -> p hc n", hc=hn)
                    dma(sf, wfo[:, h0:h0 + hn, jg * 512:(jg + 1) * 512], "stgf%d" % i, (), [("stgf", i)])
                    sbv = stg_b[i][:, 0:4 * hn * 128].rearrange("p (b hc c) -> p b hc c", b=4, hc=hn)
                    for hc in range(hn):
                        cast(sbv[:, :, hc, :], sf[:, hc, :].rearrange("p (b c) -> p b c", b=4), None,
                             [("stgf", i)], [("stgb", i)], hc)
                    dma(WfoS[l, jg * 4:(jg + 1) * 4, :, h0 * 128:(h0 + hn) * 128].rearrange("b p f -> p b f"),
                        stg_b[i][:, 0:4 * hn * 128].rearrange("p (b f) -> p b f", b=4),
                        "stgb%d" % i, [("stgb", i)], [("scr", l, i)])
        P.barrier()
        scr_reads = [("scr", l, i) for l in LAYERS for i in range(2)] if (dbg or {}).get("pre", True) else []

        only = dbg or {}
        nrm_sq = [NRM[:, 0:512], NRM[:, 512:1024]]
        nrm_ln = NRM[:, 1024:2048].bitcast(F32)
        nrm_ex = NRM[:, 2048:2560].bitcast(F32)[:, 0:256]

        def gcols(g):
            return slice(g * 512, (g + 1) * 512)

        def rms_stats(g, nb):
            for kc in range(8):
                sq = nrm_sq[kc % 2]
                act(sq, XT[:, kc, gcols(g)], AF.Square, [("X", kc, g)], [("nsq", kc % 2)])
                mm(PS[nb][:, :], ones_bf, sq, kc == 0, kc == 7, [("nsq", kc % 2), "CB"], [psk(nb)])
            act(nrm_ln, PS[nb][:, :], AF.Ln, [psk(nb), "SM"], ["nln"], bias=EPSC, scale=1.0 / D)
            act(nrm_ln, nrm_ln, AF.Exp, ["nln"], ["nln"], scale=-0.5)

        def norm_to_hT():
            for g in range(4):
                rms_stats(g, 7)
                for kc in range(8):
                    tt("dve" if kc % 2 == 0 else "pool", hT[:, kc, gcols(g)], XT[:, kc, gcols(g)], nrm_ln, ALU.mult,
                       [("X", kc, g), "nln"], [("h", kc, g)])

        def load_x(s):
            A.reset()
            xin = [A.f32(1024, "xin%d" % i) for i in range(2)]
            for t in range(16):
                i = t % 2
                dma(xin[i], x[s, t * 128:(t + 1) * 128, :], "xin%d" % i, (), [("xin", i)])
                for hb in range(2):
                    b = (2 * t + hb) % 4
                    for k4 in range(4):
                        kc = hb * 4 + k4
                        tr(PS[b][:, k4 * 128:(k4 + 1) * 128], xin[i][:, kc * 128:(kc + 1) * 128], ident_f,
                           [("xin", i), "CST"], [psk(b)])
                    cp("act" if hb == 0 else "dve", XT[:, hb * 4:hb * 4 + 4, t * 128:(t + 1) * 128],
                       PS[b][:, :].rearrange("p (k c) -> p k c", k=4), [psk(b)],
                       [("X", hb * 4 + k4, t // 4) for k4 in range(4)])

        def store_out(s):
            A.reset()
            og = [A.f32(4096, "og%d" % i) for i in range(2)]
            on = [A.f32(512, "on%d" % i) for i in range(2)]
            for g in range(4):
                rms_stats(g, 7)
                ogv = og[g % 2].rearrange("p (t d) -> p t d", t=4)
                for kc in range(8):
                    o = on[kc % 2]
                    stt(o, XT[:, kc, gcols(g)], VT[:, V_GFIN + kc:V_GFIN + kc + 1], nrm_ln, ALU.mult, ALU.mult,
                        [("X", kc, g), "nln", "VT"], [("on", kc % 2)])
                    b = kc % 4
                    for t4 in range(4):
                        tr(PS[b][:, t4 * 128:(t4 + 1) * 128], o[:, t4 * 128:(t4 + 1) * 128], ident_f,
                           [("on", kc % 2), "CST"], [psk(b)])
                    cp("act" if kc % 2 == 0 else "dve", ogv[:, :, kc * 128:(kc + 1) * 128],
                       PS[b][:, :].rearrange("p (t c) -> p t c", t=4), [psk(b)], [("og", g % 2)])
                dma(out[s, g * 512:(g + 1) * 512, :].rearrange("(t p) d -> p t d", p=128), ogv, "og%d" % (g % 2),
                    [("og", g % 2)], [("outd", s, g)])

        def load_w(tile, src, key, name):
            dma(tile, src, key, scr_reads, [name])

        def sigm_act(src_ps, tA, tAkey, reads, dst=None, dkey=None):
            act(tA, src_ps, AF.Exp, reads, [tAkey], scale=-1.0)
            act(tA, tA, AF.Ln, [tAkey, "SM"], [tAkey], bias=ONEC, scale=1.0)
            if dst is None:
                act(tA, tA, AF.Exp, [tAkey], [tAkey], scale=-1.0)
            else:
                act(dst, tA, AF.Exp, [tAkey], [dkey], scale=-1.0)

        def proj_fm(dstT, W, dkey, wname, banks, evac_eng="act", mode="copy", tmps=None):
            for g in range(4):
                b = banks[g % len(banks)]
                for kc in range(8):
                    mm(PS[b][:, :], W[:, kc * 128:(kc + 1) * 128], hT[:, kc, gcols(g)], kc == 0, kc == 7,
                       [wname, ("h", kc, g)], [psk(b)])
                if mode == "copy":
                    cp(evac_eng, dstT[:, gcols(g)], PS[b][:, :], [psk(b)], [(dkey, g)])
                elif mode == "split":
                    cp("act", dstT[0][0:64, gcols(g)], PS[b][0:64, :], [psk(b)], [(dkey, 0, g)])
                    cp("dve", dstT[1][64:128, gcols(g)], PS[b][64:128, :], [psk(b)], [(dkey, 1, g)])
                elif mode == "sigmoid":
                    tA, kA = tmps[g % 2]
                    sigm_act(PS[b][:, :], tA, kA, [psk(b)], dst=dstT[:, gcols(g)], dkey=(dkey, g))
                else:
                    tA, kA = tmps[g % 2]
                    sigm_act(PS[b][:, :], tA, kA, [psk(b)])
                    tt("dve", dstT[:, gcols(g)], PS[b][:, :], tA, ALU.mult, [psk(b), kA], [(dkey, g)])

        def proj_tm(dst3, W, dkey, wname, banks, rows=128):
            nblk = S // rows
            for bg in range(nblk // 4):
                b = banks[bg % len(banks)]
                for b4 in range(4):
                    blk = bg * 4 + b4
                    g = (blk * rows) // 512
                    for kc in range(8):
                        mm(PS[b][0:rows, b4 * 128:(b4 + 1) * 128], hT[:, kc, blk * rows:(blk + 1) * rows],
                           W[:, kc * 128:(kc + 1) * 128], kc == 0, kc == 7, [wname, ("h", kc, g)], [psk(b)])
                yield bg, b

        def da_head(l, h):
            A.reset()
            Wq = A.bf(1024, "Wq"); Wk = A.bf(1024, "Wk"); Wv = A.bf(1024, "Wv")
            QTs = [A.bf(2048, "QT0"), A.bf(2048, "QT1")]
            KT = A.bf(2048, "KT"); Vt = A.bf(2048, "V")
            V3 = Vt.rearrange("p (k c) -> p k c", c=128)
            PTs = [A.bf(512, "PT%d" % i) for i in range(3)]
            tmpf = [A.f32(256, "tmpf%d" % i) for i in range(2)]
            Rr = A.f32(512, "Rr"); Tt = A.f32(512, "Tt"); of = A.f32(256, "of")
            sqb = A.bf(256, "sqb"); rst = A.f32(256, "rst")
            load_w(Wq, WinS[l, 0 + h], "wq", "Wq")
            load_w(Wk, WinS[l, 4 + h], "wk", "Wk")
            load_w(Wv, WinS[l, 8 + h], "wv", "Wv")
            for c in range(2):
                memset("pool", QTs[c], 0.0, [("QT", c, g) for g in range(4)])
            proj_fm(QTs, Wq, "QT", "Wq", (6, 7), mode="split")
            proj_fm(KT, Wk, "KT", "Wk", (6, 7), "dve")
            for bg, b in proj_tm(V3, Wv, "V", "Wv", (5, 6)):
                cp("act" if bg % 2 == 0 else "dve", V3[:, bg * 4:(bg + 1) * 4, :],
                   PS[b][:, :].rearrange("p (k c) -> p k c", k=4), [psk(b)], [("V", bg)])
            cbias = RB[:, 15 * 4 + h:15 * 4 + h + 1]
            its = [(G, kb) for G in range(8) for kb in range(2 * G + 2)]
            n = len(its)
            deferred = []

            def S0(t):
                G, kb = its[t]
                zb = t % 2
                for c in range(2):
                    mm(PS[zb][:, c * 256:(c + 1) * 256], KT[:, kb * 128:(kb + 1) * 128],
                       QTs[c][:, G * 256:(G + 1) * 256], True, True,
                       [("KT", kb // 4), ("QT", c, G // 2)], [psk(zb)])

            def S1(t):
                G, kb = its[t]
                zb = t % 2
                pb = t % 3
                PT = PTs[pb]
                ST4 = PS[zb][:, :].rearrange("p (c j q) -> p c j q", c=2, j=2)
                PT4 = PT.rearrange("p (c j q) -> p c j q", c=2, j=2)
                rels = [kb - (2 * G + j) for j in range(2)]
                if rels[0] <= -2 and rels[1] <= -2:
                    act(PT, PS[zb][:, :], AF.Exp, [psk(zb), "RB"], [("PT", pb)], bias=cbias, scale=0.125)
                    return
                for j in range(2):
                    rel = rels[j]
                    if rel <= -2:
                        act(PT4[:, :, j, :], ST4[:, :, j, :], AF.Exp, [psk(zb), "RB"], [("PT", pb)],
                            bias=cbias, scale=0.125)
                    elif rel == 1:
                        memset("pool", PT4[:, :, j, :], 0.0, [("PT", pb)])
                    else:
                        typ = 0 if rel == 0 else 1
                        tf = tmpf[j].rearrange("p (c q) -> p c q", c=2)
                        stt(tf, ST4[:, :, j, :], 0.125, BT[:, typ, h, :, :], ALU.mult, ALU.add,
                            [psk(zb), "BT"], [("tmpf", j)])
                        act(PT4[:, :, j, :], tf, AF.Exp, [("tmpf", j)], [("PT", pb)])

            def S2(t, step):
                G, kb = its[t]
                pb = t % 3
                last = 2 * G + 1
                ob = 2 + 2 * (G % 2)
                sbk = ob + 1
                PT = PTs[pb]
                mm(PS[ob][:, :], V3[:, kb, :], PT, kb == 0, kb == last, [("V", kb // 4), ("PT", pb)], [psk(ob)])
                mm(PS[sbk][:, :], ones_bf, PT, kb == 0, kb == last, ["CB", ("PT", pb)], [psk(sbk)])
                if kb == last:
                    act(Rr, PS[sbk][:, :], AF.Ln, [psk(sbk)], ["Rr"])
                    act(Rr, Rr, AF.Exp, ["Rr"], ["Rr"], scale=-1.0)
                    tt("dve", Tt, PS[ob][:, :], Rr, ALU.mult, [psk(ob), "Rr"], ["Tt"])
                    stt(of, Tt[:, 256:512], NEGLAM[:, l:l + 1], Tt[:, 0:256], ALU.mult, ALU.add, ["Tt", "SM"], ["of"])
                    act(sqb, of, AF.Square, ["of"], ["sqb"])

                    def fin(G=G):
                        mm(PS[6][:, 0:256], ones_bf, sqb, True, True, ["CB", "sqb"], [psk(6)])
                        act(rst, PS[6][:, 0:256], AF.Ln, [psk(6), "SM"], ["rst"], bias=EPSC, scale=1.0 / 128)
                        act(rst, rst, AF.Exp, ["rst"], ["rst"], scale=-0.5)
                        stt(yT[:, h, G * 256:(G + 1) * 256], of, GSDA[:, l:l + 1], rst, ALU.mult, ALU.mult,
                            ["of", "rst", "SM"], [("y", h, G // 2)])
                    deferred.append((step + 2, fin))

            for step in range(n + 4):
                if step < n:
                    S0(step)
                    S1(step)
                if 0 <= step - 1 < n:
                    S2(step - 1, step)
                while deferred and deferred[0][0] <= step:
                    deferred.pop(0)[1]()
            while deferred:
                deferred.pop(0)[1]()

        def sb_pair(l, hp):
            A.reset()
            Wq = A.bf(1024, "Wq"); Wk = A.bf(1024, "Wk"); Wv = A.bf(1024, "Wv")
            QTs = [A.bf(2048, "QT0"), A.bf(2048, "QT1")]
            KT = A.bf(2048, "KT"); Vp = A.bf(4096, "Vp")
            Vp4 = Vp.rearrange("p (k e c) -> p k e c", k=16, e=2)
            Ef = [A.f32(512, "Ef%d" % i) for i in range(3)]
            Lb = [A.bf(512, "Lb%d" % i) for i in range(3)]
            Rb = [A.bf(512, "Rb%d" % i) for i in range(2)]
            Ex = [A.f32(512, "Ex%d" % i) for i in range(2)]
            wT = [A.bf(512, "wT%d" % i) for i in range(2)]
            load_w(Wq, WinS[l, 28 + hp], "wq", "Wq")
            load_w(Wk, WinS[l, 32 + hp], "wk", "Wk")
            load_w(Wv, WinS[l, 36 + hp], "wv", "Wv")
            for c in range(2):
                memset("pool", QTs[c], 0.0, [("QT", c, g) for g in range(4)])
            proj_fm(QTs, Wq, "QT", "Wq", (6, 7), mode="split")
            proj_fm(KT, Wk, "KT", "Wk", (6, 7), "dve")
            memset("pool", Vp, 0.0, [("V", bg) for bg in range(4)])
            for bg, b in proj_tm(None, Wv, "V", "Wv", (4, 5)):
                src = PS[b][:, :].rearrange("p (k c) -> p k c", k=4)
                cp("act", Vp4[:, bg * 4:(bg + 1) * 4, 0, 0:64], src[:, :, 0:64], [psk(b)], [("V", bg)])
                cp("dve", Vp4[:, bg * 4:(bg + 1) * 4, 1, 64:128], src[:, :, 64:128], [psk(b)], [("V", bg)])
            its = [(G, kb) for G in range(8) for kb in range(2 * G + 1, -1, -1)]
            n = len(its)

            def S0(t):
                G, kb = its[t]
                zb = t % 2
                for e in range(2):
                    mm(PS[zb][:, e * 256:(e + 1) * 256], KT[:, kb * 128:(kb + 1) * 128],
                       QTs[e][:, G * 256:(G + 1) * 256], True, True,
                       [("KT", kb // 4), ("QT", e, G // 2)], [psk(zb)])

            def S1(t):
                G, kb = its[t]
                i2 = t % 2
                i3 = t % 3
                E = Ef[i3]
                act(E, PS[i2][:, :], AF.Exp, [psk(i2)], [("Ef", i3)], scale=0.125)
                E4 = E.rearrange("p (e j q) -> p e j q", e=2, j=2)
                for j in range(2):
                    rel = kb - (2 * G + j)
                    if rel == 1:
                        memset("pool", E4[:, :, j, :], 0.0, [("Ef", i3)])
                    elif rel == 0:
                        tt("dve", E4[:, :, j, :], E4[:, :, j, :], TRI2[:], ALU.mult, [("Ef", i3), "TRI2"], [("Ef", i3)])
                act(Lb[i3], E, AF.Ln, [("Ef", i3), "SM"], [("Lb", i3)], bias=ONEC, scale=1.0)

            def S2(t):
                G, kb = its[t]
                i2 = t % 2
                i3 = t % 3
                tb = 2 + i2
                isfirst = kb == 2 * G + 1
                mm(PS[tb][:, :], uinc_bf, Lb[i3], True, isfirst, ["CB", ("Lb", i3)], [psk(tb)])
                if not isfirst:
                    mm(PS[tb][:, :], ones_bf, Rb[i2], False, True, ["CB", ("Rb", i2)], [psk(tb)])
                if kb > 0:
                    if isfirst:
                        cp("pool", Rb[1 - i2], Lb[i3], [("Lb", i3)], [("Rb", 1 - i2)])
                    else:
                        tt("pool", Rb[1 - i2], Rb[i2], Lb[i3], ALU.add, [("Lb", i3), ("Rb", i2)], [("Rb", 1 - i2)])

            def S3(t):
                G, kb = its[t]
                i2 = t % 2
                i3 = t % 3
                tb = 2 + i2
                act(Ex[i2], PS[tb][:, :], AF.Exp, [psk(tb)], [("Ex", i2)], scale=-1.0)
                tt("dve", wT[i2], Ef[i3], Ex[i2], ALU.mult, [("Ef", i3), ("Ex", i2)], [("wT", i2)])

            def S4(t):
                G, kb = its[t]
                i2 = t % 2
                ob = 4 + G % 2
                isfirst = kb == 2 * G + 1
                for e in range(2):
                    mm(PS[ob][:, 0:256], Vp4[:, kb, e, :], wT[i2][:, e * 256:(e + 1) * 256],
                       isfirst and e == 0, kb == 0 and e == 1, [("V", kb // 4), ("wT", i2)], [psk(ob)])
                if kb == 0:
                    cp("act", yT[:, hp, G * 256:(G + 1) * 256], PS[ob][:, 0:256], [psk(ob)], [("y", hp, G // 2)])

            for step in range(n + 2):
                if step < n:
                    S0(step)
                    S1(step)
                if 0 <= step - 1 < n:
                    S2(step - 1)
                    S3(step - 1)
                if 0 <= step - 2 < n:
                    S4(step - 2)

        def hg_head(l, hh):
            A.reset()
            Wf = A.bf(1024, "Wf"); Wi = A.bf(1024, "Wi"); Wq = A.bf(1024, "Wq"); Wg = A.bf(1024, "Wg")
            SIG = A.f32(2048, "SIG")
            qA = A.bf(2048, "qA"); GS = A.bf(2048, "GS"); kt = A.bf(2048, "kt"); kdT = A.bf(2048, "kdT")
            Vh = A.bf(4096, "Vh")
            Vh3 = Vh.rearrange("p (c v) -> p c v", v=128)
            ELt = A.f32(32, "ELt")
            tf = [A.f32(512, "t%d" % i) for i in range(6)]
            scT = [A.bf(512, "scT%d" % i) for i in range(2)]
            kd = [A.bf(1024, "kd%d" % i) for i in range(2)]
            stf = A.f32(128, "stf")
            stb = [A.bf(128, "stb%d" % i) for i in range(2)]
            sqb = A.bf(512, "sqb"); rst = A.f32(512, "rst"); y1 = A.f32(512, "y1")
            load_w(Wf, WinS[l, 12 + hh], "wq", "Wf")
            load_w(Wi, WinS[l, 16 + hh], "wk", "Wi")
            load_w(Wq, WinS[l, 20 + hh], "wv", "Wq")
            load_w(Wg, WinS[l, 24 + hh], "wg", "Wg")
            tmps = ((tf[0], "t0"), (tf[1], "t1"))
            proj_fm(SIG, Wf, "SIG", "Wf", (0, 1), mode="sigmoid", tmps=tmps)
            proj_fm(qA, Wq, "qA", "Wq", (0, 1), mode="silu", tmps=tmps)
            proj_fm(GS, Wg, "GS", "Wg", (0, 1), mode="silu", tmps=tmps)
            for bg, b in proj_tm(None, Wi, "Vh", "Wi", (2, 3), rows=64):
                cp("dve" if bg % 2 == 0 else "act", Vh3[0:64, bg * 4:(bg + 1) * 4, :],
                   PS[b][0:64, :].rearrange("p (k c) -> p k c", k=4), [psk(b)], [("Vh", bg // 2)])
            lbc = LBV[:, l * 4 + hh:l * 4 + hh + 1]
            omc = OMLB[:, l * 4 + hh:l * 4 + hh + 1]
            for g in range(4):
                f, lf, kg, cum, Aex, Bex = tf
                ts("dve", f, SIG[:, gcols(g)], omc, lbc, ALU.mult, ALU.add, [("SIG", g), "SM"], ["t0"])
                act(lf, f, AF.Ln, ["t0"], ["t1"])
                ts("pool", kg, f, -1.0, 1.0, ALU.mult, ALU.add, ["t0"], ["t2"])
                P.add("dve", lambda e, cum=cum, lf=lf: e.tensor_tensor_scan(
                    out=cum, data0=CST[:, C_RMSK:C_RMSK + 512], data1=lf, initial=0.0, op0=ALU.mult, op1=ALU.add),
                    ["t1", "CST"], ["t3"])
                act(Aex, cum, AF.Exp, ["t3"], ["t4"])
                act(Bex, cum, AF.Exp, ["t3"], ["t5"], scale=-1.0)
                cp("pool", ELt[:, g * 8:(g + 1) * 8], Aex.rearrange("p (c t) -> p c t", t=64)[:, :, 63], ["t4"], ["ELt"])
                tt("dve", qA[:, gcols(g)], qA[:, gcols(g)], Aex, ALU.mult, [("qA", g), "t4"], [("qA", g)])
                tt("dve", kg, kg, Bex, ALU.mult, ["t2", "t5"], ["t2"])
                cp("pool", kt[:, gcols(g)], kg, ["t2"], [("kt", g)])
                for c8 in range(8):
                    c = g * 8 + c8
                    ts("pool" if c8 % 2 == 0 else "dve", kdT[:, c * 64:(c + 1) * 64], kg[:, c8 * 64:(c8 + 1) * 64],
                       ELt[:, c:c + 1], 0.0, ALU.mult, ALU.add, ["t2", "ELt"], [("kdT", g)])
            hgm = CST[0:64, C_HGM:C_HGM + 512]
            for cg in range(4):
                i2 = cg % 2
                ob = 4 + cg % 2
                for c8 in range(8):
                    c = cg * 8 + c8
                    mm(PS[2][0:64, c8 * 64:(c8 + 1) * 64], kt[:, c * 64:(c + 1) * 64], qA[:, c * 64:(c + 1) * 64],
                       True, True, [("kt", cg), ("qA", cg)], [psk(2)])
                tt("dve", scT[i2][0:64, :], PS[2][0:64, :], hgm, ALU.mult, [psk(2), "CST"], [("scT", i2)])
                psb = PS[3][:, :].bitcast(BF16)
                for c8 in range(8):
                    c = cg * 8 + c8
                    tr(psb[0:64, c8 * 128:(c8 + 1) * 128], kdT[:, c * 64:(c + 1) * 64], ident_bf,
                       [("kdT", cg), "CB"], [psk(3)])
                cp("act", kd[i2][0:64, :], psb[0:64, :], [psk(3)], [("kd", i2)])
                kd3 = kd[i2].rearrange("p (c d) -> p c d", d=128)
                for c8 in range(8):
                    c = cg * 8 + c8
                    mm(PS[ob][:, c8 * 64:(c8 + 1) * 64], Vh3[0:64, c, :], scT[i2][0:64, c8 * 64:(c8 + 1) * 64],
                       True, c == 0, [("Vh", c // 8), ("scT", i2)], [psk(ob)])
                    if c > 0:
                        mm(PS[ob][:, c8 * 64:(c8 + 1) * 64], stb[(c - 1) % 2], qA[:, c * 64:(c + 1) * 64],
                           False, True, [("stb", (c - 1) % 2), ("qA", cg)], [psk(ob)])
                    if c < 31:
                        mm(PS[6][:, 0:128], kd3[0:64, c8, :], Vh3[0:64, c, :], True, True,
                           [("kd", i2), ("Vh", c // 8)], [psk(6)])
                        if c == 0:
                            cp("dve", stf, PS[6][:, 0:128], [psk(6)], ["stf"])
                        else:
                            stt(stf, stf, ELt[:, c:c + 1], PS[6][:, 0:128], ALU.mult, ALU.add,
                                ["stf", "ELt", psk(6)], ["stf"])
                        cp("pool", stb[c % 2], stf, ["stf"], [("stb", c % 2)])
                act(sqb, PS[ob][:, :], AF.Square, [psk(ob)], ["sqb"])
                mm(PS[7][:, :], ones_bf, sqb, True, True, ["CB", "sqb"], [psk(7)])
                act(rst, PS[7][:, :], AF.Ln, [psk(7), "SM"], ["rst"], bias=EPSC, scale=1.0 / 128)
                act(rst, rst, AF.Exp, ["rst"], ["rst"], scale=-0.5)
                tt("dve", y1, PS[ob][:, :], rst, ALU.mult, [psk(ob), "rst"], ["y1"])
                stt(yT[:, hh, gcols(cg)], y1, VT[:, V_HGN + l:V_HGN + l + 1], GS[:, gcols(cg)], ALU.mult, ALU.mult,
                    ["y1", "VT", ("GS", cg)], [("y", hh, cg)])

        def merge(l, n):
            A.reset()
            Wu = A.bf(4096, "Wu"); Wg = A.bf(8192, "Wg"); Wo = A.bf(8192, "Wo")
            Wu3 = Wu.rearrange("p (kc c) -> p kc c", kc=4)
            Wg4 = Wg.rearrange("p (j kc c) -> p j kc c", j=8, kc=8)
            Wo3 = Wo.rearrange("p (kc c) -> p kc c", kc=8)
            mT = [A.bf(4096, "mT%d" % i) for i in range(2)]
            sg = [A.f32(512, "sg%d" % i) for i in range(2)]
            load_w(Wu, WupS[l, n], "wq", "Wu")
            load_w(Wg.rearrange("p (j f) -> p j f", j=8), WinS[l, 40 + 8 * n:48 + 8 * n].rearrange("b p f -> p b f"),
                   "wk", "Wg")
            load_w(Wo, WoutS[l], "wv", "Wo")
            it = 0
            for g in range(4):
                m3 = mT[g % 2].rearrange("p (j t) -> p j t", j=8)
                for j in range(8):
                    ub = it % 2
                    gb = 2 + it % 2
                    it += 1
                    for kc in range(4):
                        mm(PS[ub][:, :], Wu3[:, kc, j * 128:(j + 1) * 128], yT[:, kc, gcols(g)], kc == 0, kc == 3,
                           ["Wu", ("y", kc, g)], [psk(ub)])
                    for kc in range(8):
                        mm(PS[gb][:, :], Wg4[:, j, kc, :], hT[:, kc, gcols(g)], kc == 0, kc == 7,
                           ["Wg", ("h", kc, g)], [psk(gb)])
                    sigm_act(PS[gb][:, :], sg[ub], ("sg", ub), [psk(gb)])
                    tt("dve", m3[:, j, :], PS[ub][:, :], sg[ub], ALU.mult, [psk(ub), ("sg", ub)], [("mT", g % 2, j)])
                for j2 in range(8):
                    ob = 4 + j2 % 2
                    for j in range(8):
                        mm(PS[ob][:, :], Wo3[:, j, j2 * 128:(j2 + 1) * 128], m3[:, j, :], j == 0, j == 7,
                           ["Wo", ("mT", g % 2, j)], [psk(ob)])
                    tt("dve", XT[:, j2, gcols(g)], XT[:, j2, gcols(g)], PS[ob][:, :], ALU.add,
                       [("X", j2, g), psk(ob)], [("X", j2, g)])

        def ffn_half(l, hf):
            A.reset()
            sT = A.bf(NHC * 1024, "sT")
            sT3 = sT.rearrange("p (hc t) -> p hc t", hc=NHC)
            Wa = [A.bf(1024, "Wa%d" % i) for i in range(2)]
            Wb = [A.bf(1024, "Wb%d" % i) for i in range(2)]
            Wo = [A.bf(FFN_H, "Wfo%d" % i) for i in range(2)]
            sg = [A.f32(512, "sg%d" % i) for i in range(2)]
            it = 0
            for hc in range(NHC):
                i2 = hc % 2
                load_w(Wa[i2], WfiS[l, hc], "wa%d" % i2, ("Wa", i2))
                load_w(Wb[i2], WfiS[l, NHC + hc], "wb%d" % i2, ("Wb", i2))
                for gg in range(2):
                    g = 2 * hf + gg
                    ab = it % 2
                    bb = 2 + it % 2
                    it += 1
                    for kc in range(8):
                        mm(PS[ab][:, :], Wa[i2][:, kc * 128:(kc + 1) * 128], hT[:, kc, gcols(g)], kc == 0, kc == 7,
                           [("Wa", i2), ("h", kc, g)], [psk(ab)])
                    for kc in range(8):
                        mm(PS[bb][:, :], Wb[i2][:, kc * 128:(kc + 1) * 128], hT[:, kc, gcols(g)], kc == 0, kc == 7,
                           [("Wb", i2), ("h", kc, g)], [psk(bb)])
                    sigm_act(PS[ab][:, :], sg[ab], ("sg", ab), [psk(ab)])
                    tt("dve", sg[ab], PS[ab][:, :], sg[ab], ALU.mult, [psk(ab), ("sg", ab)], [("sg", ab)])
                    tt("dve", sT3[:, hc, gg * 512:(gg + 1) * 512], sg[ab], PS[bb][:, :], ALU.mult,
                       [("sg", ab), psk(bb)], [("sT", hc, gg)])
            for j in range(8):
                i2 = j % 2
                load_w(Wo[i2], WfoS[l, j], "wo%d" % i2, ("Wfo", i2))
                for gg in range(2):
                    g = 2 * hf + gg
                    ob = 4 + (2 * j + gg) % 2
                    for hc in range(NHC):
                        mm(PS[ob][:, :], Wo[i2][:, hc * 128:(hc + 1) * 128], sT3[:, hc, gg * 512:(gg + 1) * 512],
                           hc == 0, hc == NHC - 1, [("Wfo", i2), ("sT", hc, gg)], [psk(ob)])
                    tt("dve", XT[:, j, gcols(g)], XT[:, j, gcols(g)], PS[ob][:, :], ALU.add,
                       [("X", j, g), psk(ob)], [("X", j, g)])

        for s in range(NSEQ):
            load_x(s)
            P.barrier()
            for l in LAYERS:
                norm_to_hT()
                P.barrier()
                if only.get("da", True):
                    for h in range(4):
                        da_head(l, h)
                        P.barrier()
                    merge(l, 0)
                    P.barrier()
                if only.get("hg", True):
                    for hh in range(4):
                        hg_head(l, hh)
                        P.barrier()
                    merge(l, 1)
                    P.barrier()
                if only.get("sb", True):
                    for hp in range(4):
                        sb_pair(l, hp)
                        P.barrier()
                    merge(l, 2)
                    P.barrier()
                if only.get("ffn", True):
                    norm_to_hT()
                    P.barrier()
                    for hf in range(2):
                        ffn_half(l, hf)
                        P.barrier()
            store_out(s)
            P.barrier()
        P.add("sp", None, reads=[("outd", s, g) for s in range(NSEQ) for g in range(4)])
        P.emit(nc)
    return nc


_W_NAMES = ["norm_mix_g", "w_in", "rel_bias", "diff_lambda", "diff_subln_g", "hgrn_lb_logits", "hgrn_norm_g",
            "w_up", "w_out", "norm_ffn_g", "w_ffn_in", "w_ffn_out", "final_norm_g"]


def kernel(**inputs):
    x = np.ascontiguousarray(np.asarray(inputs["x"], dtype=np.float32))
    B = x.shape[0]
    nseq = B // NCORES
    shared = {k: np.ascontiguousarray(np.asarray(inputs[k], dtype=np.float32)) for k in _W_NAMES}
    shared["cst"] = make_consts()
    nc = build(NSEQ=nseq)
    in_maps = []
    for c in range(NCORES):
        m = dict(shared)
        m["x"] = x[c * nseq:(c + 1) * nseq]
        in_maps.append(m)
    res = run_bass_kernel_spmd(nc, in_maps, core_ids=list(range(NCORES)))
    return np.concatenate([np.asarray(r["out"], dtype=np.float32) for r in res.results], axis=0)
```
